# Optimizing a Trainium2 kernel written in Bass

```python
import math
import jax, jax.numpy as jnp
from jax import lax
import numpy as np

D_MODEL = 1024
BATCH = 8
SEQ = 4096
DEPTH = 2

N_A_LAYERS = (DEPTH + 1) // 2
N_B_LAYERS = DEPTH - N_A_LAYERS
HEAD_DIM = 64
MEM_HEADS = 4
MEM_WIDTH = MEM_HEADS * HEAD_DIM
PRIMARY_WIDTH = D_MODEL - MEM_WIDTH
B_HEADS = PRIMARY_WIDTH // HEAD_DIM
CONV_CH = PRIMARY_WIDTH
CONV_WIDTH = 31
D_FF = 256 * ((8 * D_MODEL // 3 + 255) // 256)
MOBA_BLOCK = 256
MOBA_TOPK = 3
Q_CHUNK = 16
ROPE_THETA = 500000.0
ROPE_DIM = HEAD_DIM // 4
EPS = 1e-6

kernel_name = 'conformer_conv_moba_yoco_hybrid'


def rms_norm(x, g):
    xf = x.astype(jnp.float32)
    y = xf * lax.rsqrt(jnp.mean(xf * xf, axis=-1, keepdims=True) + EPS)
    return (y * g.astype(jnp.float32)).astype(x.dtype)


def layer_norm(x, g, b):
    xf = x.astype(jnp.float32)
    xc = xf - jnp.mean(xf, axis=-1, keepdims=True)
    y = xc * lax.rsqrt(jnp.mean(xc * xc, axis=-1, keepdims=True) + EPS)
    return (y * g.astype(jnp.float32) + b.astype(jnp.float32)).astype(x.dtype)


def swiglu(x, w_gate, w_up, w_down):
    return (jax.nn.silu(x @ w_gate) * (x @ w_up)) @ w_down


def partial_rope(x, positions):
    half = ROPE_DIM // 2
    inv_freq = 1.0 / (ROPE_THETA ** (jnp.arange(0, ROPE_DIM, 2, dtype=jnp.float32) / ROPE_DIM))
    ang = positions.astype(jnp.float32)[..., None] * inv_freq
    cos = jnp.cos(ang)[:, :, None, :]
    sin = jnp.sin(ang)[:, :, None, :]
    xr = x[..., :ROPE_DIM].astype(jnp.float32)
    x1, x2 = xr[..., :half], xr[..., half:]
    rot = jnp.concatenate([x1 * cos - x2 * sin, x2 * cos + x1 * sin], axis=-1)
    return jnp.concatenate([rot.astype(x.dtype), x[..., ROPE_DIM:]], axis=-1)


def conformer_conv(u, dw_kernel, dw_bias, ln_g, ln_b):
    a, gate = jnp.split(u, 2, axis=-1)
    h = a * jax.nn.sigmoid(gate)
    h = lax.conv_general_dilated(h, dw_kernel.astype(h.dtype), window_strides=(1,),
                                 padding=[(CONV_WIDTH - 1, 0)],
                                 dimension_numbers=('NWC', 'WIO', 'NWC'),
                                 feature_group_count=CONV_CH) + dw_bias
    return jax.nn.silu(layer_norm(h, ln_g, ln_b))


def memory_attention(q, mem_n, w_mem_kv, q_g, k_g):
    B, S = q.shape[:2]
    M = mem_n.shape[1]
    k, v = jnp.split(mem_n @ w_mem_kv, 2, axis=-1)
    k = rms_norm(k.reshape(B, M, MEM_HEADS, HEAD_DIM), k_g)
    v = v.reshape(B, M, MEM_HEADS, HEAD_DIM)
    q = rms_norm(q.reshape(B, S, MEM_HEADS, HEAD_DIM), q_g)
    logits = jnp.einsum('bshd,bmhd->bhsm', q, k).astype(jnp.float32) * (HEAD_DIM ** -0.5)
    p = jax.nn.softmax(logits, axis=-1).astype(v.dtype)
    return jnp.einsum('bhsm,bmhd->bshd', p, v).reshape(B, S, MEM_WIDTH)


def shared_moba_kv(h, positions, kv_norm_g, w_kv, k_norm_g):
    B, S, _ = h.shape
    k, v = jnp.split(rms_norm(h, kv_norm_g) @ w_kv, 2, axis=-1)
    k = partial_rope(rms_norm(k.reshape(B, S, B_HEADS, HEAD_DIM), k_norm_g), positions)
    v = v.reshape(B, S, B_HEADS, HEAD_DIM)
    n_blocks = -(-S // MOBA_BLOCK)
    pad = n_blocks * MOBA_BLOCK - S

    def to_blocks(t):
        t = jnp.pad(t, ((0, 0), (0, pad), (0, 0), (0, 0)))
        return t.reshape(B, n_blocks, MOBA_BLOCK, B_HEADS, HEAD_DIM).transpose(0, 3, 1, 2, 4)

    k_blocks = to_blocks(k)
    v_blocks = to_blocks(v)
    k_mean = jnp.mean(k_blocks.astype(jnp.float32), axis=3).astype(k_blocks.dtype)
    return k_blocks, v_blocks, k_mean


def moba_attention(q, k_blocks, v_blocks, k_mean):
    B, S, H, Dh = q.shape
    n_blocks = k_blocks.shape[2]
    topk = min(MOBA_TOPK, n_blocks)
    n_chunks = S // Q_CHUNK
    scale = Dh ** -0.5
    q_chunks = q.reshape(B, n_chunks, Q_CHUNK, H, Dh).transpose(1, 0, 3, 2, 4)
    b_idx = jnp.arange(B)[:, None, None, None]
    h_idx = jnp.arange(H)[None, :, None, None]

    def chunk_attend(args):
        q_c, c = args
        start = c * Q_CHUNK
        own = start // MOBA_BLOCK
        gate = jnp.einsum('bhqd,bhnd->bhqn', q_c, k_mean).astype(jnp.float32)
        gate = jnp.where(jnp.arange(n_blocks) < own, gate, -jnp.inf)
        _, sel = lax.top_k(gate, topk)
        valid = jnp.arange(topk) < own
        k_sel = k_blocks[b_idx, h_idx, sel]
        v_sel = v_blocks[b_idx, h_idx, sel]
        k_own = lax.dynamic_index_in_dim(k_blocks, own, axis=2, keepdims=False)
        v_own = lax.dynamic_index_in_dim(v_blocks, own, axis=2, keepdims=False)
        s_sel = jnp.einsum('bhqd,bhqnjd->bhqnj', q_c, k_sel).astype(jnp.float32) * scale
        s_sel = jnp.where(valid[:, None], s_sel, -jnp.inf)
        s_own = jnp.einsum('bhqd,bhjd->bhqj', q_c, k_own).astype(jnp.float32) * scale
        q_pos = start + jnp.arange(Q_CHUNK)
        k_pos = own * MOBA_BLOCK + jnp.arange(MOBA_BLOCK)
        s_own = jnp.where(k_pos[None, :] <= q_pos[:, None], s_own, -jnp.inf)
        logits = jnp.concatenate([s_sel.reshape(B, H, Q_CHUNK, topk * MOBA_BLOCK), s_own], axis=-1)
        p = jax.nn.softmax(logits, axis=-1).astype(v_blocks.dtype)
        p_sel = p[..., :topk * MOBA_BLOCK].reshape(B, H, Q_CHUNK, topk, MOBA_BLOCK)
        p_own = p[..., topk * MOBA_BLOCK:]
        return (jnp.einsum('bhqnj,bhqnjd->bhqd', p_sel, v_sel)
                + jnp.einsum('bhqj,bhjd->bhqd', p_own, v_own))

    out = lax.map(chunk_attend, (q_chunks, jnp.arange(n_chunks)))
    return out.transpose(1, 0, 3, 2, 4).reshape(B, S, H * Dh)


def setup_inputs(seed: int = 0) -> dict:
    key = jax.random.key(seed)
    it = iter(jax.random.split(key, 64))

    def w(shape, fan_in):
        return jax.random.normal(next(it), shape, jnp.float32) * fan_in ** -0.5

    def gain(shape):
        return 1.0 + 0.01 * jax.random.normal(next(it), shape, jnp.float32)

    def bias(shape):
        return 0.01 * jax.random.normal(next(it), shape, jnp.float32)

    x = jax.random.normal(next(it), (BATCH, SEQ, D_MODEL), jnp.float32)
    mem = jax.random.normal(next(it), (BATCH, 256, D_MODEL), jnp.float32)
    offsets = jax.random.randint(next(it), (BATCH, 1), 0, 8192, dtype=jnp.int32)
    positions = (offsets + jnp.arange(SEQ, dtype=jnp.int32)[None, :]).astype(jnp.int32)
    return {
        'x': x, 'mem': mem, 'positions': positions,
        'ffn1_norm_g': gain((DEPTH, D_MODEL)),
        'ffn1_w_gate': w((DEPTH, D_MODEL, D_FF), D_MODEL),
        'ffn1_w_up': w((DEPTH, D_MODEL, D_FF), D_MODEL),
        'ffn1_w_down': w((DEPTH, D_FF, D_MODEL), D_FF),
        'mix_norm_g': gain((DEPTH, D_MODEL)),
        'mem_norm_g': gain((DEPTH, D_MODEL)),
        'w_mem_kv': w((DEPTH, D_MODEL, 2 * MEM_WIDTH), D_MODEL),
        'mem_q_norm_g': gain((DEPTH, HEAD_DIM)),
        'mem_k_norm_g': gain((DEPTH, HEAD_DIM)),
        'w_o': w((DEPTH, PRIMARY_WIDTH + MEM_WIDTH, D_MODEL), PRIMARY_WIDTH + MEM_WIDTH),
        'ffn2_norm_g': gain((DEPTH, D_MODEL)),
        'ffn2_w_gate': w((DEPTH, D_MODEL, D_FF), D_MODEL),
        'ffn2_w_up': w((DEPTH, D_MODEL, D_FF), D_MODEL),
        'ffn2_w_down': w((DEPTH, D_FF, D_MODEL), D_FF),
        'a_w_in': w((N_A_LAYERS, D_MODEL, 2 * CONV_CH + MEM_WIDTH), D_MODEL),
        'a_dw_kernel': w((N_A_LAYERS, CONV_WIDTH, 1, CONV_CH), CONV_WIDTH),
        'a_dw_bias': bias((N_A_LAYERS, CONV_CH)),
        'a_ln_g': gain((N_A_LAYERS, CONV_CH)),
        'a_ln_b': bias((N_A_LAYERS, CONV_CH)),
        'kv_norm_g': gain((D_MODEL,)),
        'w_kv': w((D_MODEL, 2 * PRIMARY_WIDTH), D_MODEL),
        'k_norm_g': gain((HEAD_DIM,)),
        'b_w_in': w((N_B_LAYERS, D_MODEL, PRIMARY_WIDTH + MEM_WIDTH), D_MODEL),
        'b_q_norm_g': gain((N_B_LAYERS, HEAD_DIM)),
    }


def reference(x, mem, positions, ffn1_norm_g, ffn1_w_gate, ffn1_w_up, ffn1_w_down,
              mix_norm_g, mem_norm_g, w_mem_kv, mem_q_norm_g, mem_k_norm_g, w_o,
              ffn2_norm_g, ffn2_w_gate, ffn2_w_up, ffn2_w_down,
              a_w_in, a_dw_kernel, a_dw_bias, a_ln_g, a_ln_b,
              kv_norm_g, w_kv, k_norm_g, b_w_in, b_q_norm_g):
    B, S, _ = x.shape
    h = x
    shared_kv = None
    for layer in range(DEPTH):
        if layer == N_A_LAYERS:
            shared_kv = shared_moba_kv(h, positions, kv_norm_g, w_kv, k_norm_g)
        h = h + 0.5 * swiglu(rms_norm(h, ffn1_norm_g[layer]), ffn1_w_gate[layer],
                             ffn1_w_up[layer], ffn1_w_down[layer])
        hn = rms_norm(h, mix_norm_g[layer])
        if layer < N_A_LAYERS:
            u = hn @ a_w_in[layer]
            primary = conformer_conv(u[..., :2 * CONV_CH], a_dw_kernel[layer], a_dw_bias[layer],
                                     a_ln_g[layer], a_ln_b[layer])
            q_mem = u[..., 2 * CONV_CH:]
        else:
            j = layer - N_A_LAYERS
            u = hn @ b_w_in[j]
            q = rms_norm(u[..., :PRIMARY_WIDTH].reshape(B, S, B_HEADS, HEAD_DIM), b_q_norm_g[j])
            k_blocks, v_blocks, k_mean = shared_kv
            primary = moba_attention(partial_rope(q, positions), k_blocks, v_blocks, k_mean)
            q_mem = u[..., PRIMARY_WIDTH:]
        mem_out = memory_attention(q_mem, rms_norm(mem, mem_norm_g[layer]), w_mem_kv[layer],
                                   mem_q_norm_g[layer], mem_k_norm_g[layer])
        h = h + jnp.concatenate([primary, mem_out], axis=-1) @ w_o[layer]
        h = h + 0.5 * swiglu(rms_norm(h, ffn2_norm_g[layer]), ffn2_w_gate[layer],
                             ffn2_w_up[layer], ffn2_w_down[layer])
    return h
```

```python
import math
from contextlib import ExitStack

import numpy as np
import concourse.bass as bass
import concourse.mybir as mybir
from concourse.bass_utils import run_bass_kernel_spmd

F32 = mybir.dt.float32
BF16 = mybir.dt.bfloat16
I32 = mybir.dt.int32
AF = mybir.ActivationFunctionType
ALU = mybir.AluOpType
AX = mybir.AxisListType

D = 1024
S = 4096
DFF = 2816
NF = DFF // 128
T = 512
NT = S // T
HD = 64
NH_B = 12
NH_M = 4
CONVW = 31
HALO = CONVW - 1
EPS = 1e-6
BIG = 30000.0
MAG = 12582912.0
NW = 10
ENGS = ("pe", "act", "dve", "pool", "sp")
import os as _os
_DBG = _os.environ.get('DBG_SKIP', '')


class Tk:
    __slots__ = ("w", "rs", "excl")

    def __init__(self, excl=False):
        self.w = None
        self.rs = []
        self.excl = excl


class V:
    __slots__ = ("ap", "tks")

    def __init__(self, ap, tks):
        self.ap = ap
        self.tks = tks

    def __getitem__(self, idx):
        return V(self.ap[idx], self.tks)

    def rearrange(self, pat, **kw):
        return V(self.ap.rearrange(pat, **kw), self.tks)


class Prog:
    def __init__(self, nc):
        self.nc = nc
        self.ops = {e: [] for e in ENGS}
        self.cnt = {}
        self.seen = {e: {} for e in ENGS}
        self.semnames = []

    def _key(self, key):
        if key not in self.cnt:
            self.cnt[key] = 0
            self.semnames.append(key)
        return key

    def _deps(self, eng, reads, writes):
        deps = {}

        def add(tok):
            if tok is None:
                return
            k, v = tok
            if deps.get(k, 0) < v:
                deps[k] = v
        for t in reads:
            add(t.w)
            if t.excl:
                for r in t.rs:
                    if r[0] != eng:
                        add(r)
        for t in writes:
            add(t.w)
            for r in t.rs:
                add(r)
        waits = []
        for k, v in deps.items():
            if k == "pe" and eng == "pe":
                continue
            if self.seen[eng].get(k, 0) >= v:
                continue
            self.seen[eng][k] = v
            waits.append((k, v))
        return waits

    def _mark(self, tok, reads, writes):
        for t in reads:
            t.rs.append(tok)
            if len(t.rs) > 64:
                best = {}
                for k, v in t.rs:
                    if best.get(k, 0) < v:
                        best[k] = v
                t.rs = list(best.items())
        for t in writes:
            t.w = tok
            t.rs = []

    def op(self, eng, fn, reads=(), writes=(), inc=True):
        waits = self._deps(eng, reads, writes)
        self._key(eng)
        if inc:
            self.cnt[eng] += 1
            tok = (eng, self.cnt[eng])
        else:
            tok = (eng, self.cnt[eng] + 1)
        self.ops[eng].append((waits, fn, (eng, 1) if inc else None))
        self._mark(tok, reads, writes)
        return tok

    def dma(self, q, chan, fn, reads=(), writes=()):
        waits = self._deps(q, reads, writes)
        key = self._key("c_" + chan)
        self.cnt[key] += 16
        tok = (key, self.cnt[key])
        self.ops[q].append((waits, fn, (key, 16)))
        self._mark(tok, reads, writes)
        return tok

    def wait_all(self, eng, toks):
        waits = []
        for k, v in toks:
            if self.seen[eng].get(k, 0) >= v:
                continue
            self.seen[eng][k] = v
            waits.append((k, v))
        self.ops[eng].append((waits, None, None))

    def emit(self, stack):
        nc = self.nc
        sems = {}
        for key in self.semnames:
            sems[key] = stack.enter_context(nc.semaphore("s_" + key))
        block = stack.enter_context(nc.Block())

        def runner(name):
            def run(eng):
                for waits, fn, inc in self.ops[name]:
                    for k, v in waits:
                        eng.wait_ge(sems[k], v)
                    if fn is not None:
                        ins = fn(eng)
                        if inc is not None:
                            ins.then_inc(sems[inc[0]], inc[1])
            return run

        block.tensor(runner("pe"))
        block.scalar(runner("act"))
        block.vector(runner("dve"))
        block.gpsimd(runner("pool"))
        block.sync(runner("sp"))


def alias_switch(old_tks, new_tks):
    pend = {}
    for t in old_tks:
        toks = list(t.rs)
        if t.w is not None:
            toks.append(t.w)
        for k, v in toks:
            if pend.get(k, 0) < v:
                pend[k] = v
    pl = list(pend.items())
    for t in new_tks:
        t.w = None
        t.rs = list(pl)


class Builder:
    def __init__(self, stop_after=None, n_tiles=NT, skip_init=False):
        self.stop_after = stop_after
        self.n_tiles = n_tiles
        self.skip_init = skip_init
        self.nc = bass.Bass("TRN2", target_bir_lowering=False)
        self.P = Prog(self.nc)
        self.st = ExitStack()
        self.rings = {}

    def dram_in(self, name, shape, dt=F32):
        return self.nc.dram_tensor(name, list(shape), dt, kind="ExternalInput").ap()

    def sb(self, name, shape, dt):
        t = self.st.enter_context(self.nc.sbuf_tensor(name, list(shape), dt))
        return V(t[:], [Tk()])

    def sbc(self, name, shape, dt):
        t = self.st.enter_context(self.nc.sbuf_tensor(name, list(shape), dt))
        tks = [Tk() for _ in range(shape[1])]
        return [V(t[:, i], [tks[i]]) for i in range(shape[1])], V(t[:], tks)

    def ring(self, name, n, shape, dt):
        self.rings[name] = [[self.sb(f"{name}{i}", shape, dt) for i in range(n)], 0]

    def nxt(self, name):
        r = self.rings[name]
        v = r[0][r[1] % len(r[0])]
        r[1] += 1
        return v

    def mm(self, out, lhsT, rhs, start=True, stop=True):
        self.P.op("pe", lambda e: e.matmul(out.ap, lhsT=lhsT.ap, rhs=rhs.ap, start=start, stop=stop),
                  reads=lhsT.tks + rhs.tks, writes=out.tks, inc=True)

    def tr(self, out, in_, ident):
        self.P.op("pe", lambda e: e.transpose(out=out.ap, in_=in_.ap, identity=ident.ap),
                  reads=in_.tks + ident.tks, writes=out.tks)

    def act(self, out, in_, func, bias=None, scale=None, accum=None):
        reads = list(in_.tks)
        kw = {}
        if bias is not None:
            if isinstance(bias, V):
                reads += bias.tks
                kw["bias"] = bias.ap
            else:
                kw["bias"] = float(bias)
        if scale is not None:
            if isinstance(scale, V):
                reads += scale.tks
                kw["scale"] = scale.ap
            else:
                kw["scale"] = float(scale)
        writes = list(out.tks)
        if accum is not None:
            kw["accum_out"] = accum.ap
            writes += accum.tks
        self.P.op("act", lambda e: e.activation(out=out.ap, in_=in_.ap, func=func, **kw),
                  reads=reads, writes=writes)

    def tt(self, out, a, b, op, eng="dve"):
        self.P.op(eng, lambda e: e.tensor_tensor(out=out.ap, in0=a.ap, in1=b.ap, op=op),
                  reads=a.tks + b.tks, writes=out.tks)

    def ts(self, out, a, s1, s2=None, op0=ALU.mult, op1=None, eng="dve"):
        reads = list(a.tks)

        def low(s):
            if isinstance(s, V):
                reads.extend(s.tks)
                return s.ap
            return None if s is None else float(s)
        a1, a2 = low(s1), low(s2)
        if op1 is None:
            self.P.op(eng, lambda e: e.tensor_scalar(out=out.ap, in0=a.ap, scalar1=a1, scalar2=None, op0=op0),
                      reads=reads, writes=out.tks)
        else:
            self.P.op(eng, lambda e: e.tensor_scalar(out=out.ap, in0=a.ap, scalar1=a1, scalar2=a2, op0=op0, op1=op1),
                      reads=reads, writes=out.tks)

    def stt(self, out, a, scalar, b, op0, op1, eng="dve"):
        reads = a.tks + b.tks
        if isinstance(scalar, V):
            reads = reads + scalar.tks
            sc = scalar.ap
        else:
            sc = float(scalar)
        self.P.op(eng, lambda e: e.scalar_tensor_tensor(out=out.ap, in0=a.ap, scalar=sc, in1=b.ap, op0=op0, op1=op1),
                  reads=reads, writes=out.tks)

    def copy(self, out, in_, eng="dve"):
        if eng == "act":
            self.act(out, in_, AF.Copy)
        else:
            self.P.op(eng, lambda e: e.tensor_copy(out=out.ap, in_=in_.ap), reads=in_.tks, writes=out.tks)

    def memset(self, out, val, eng="dve"):
        self.P.op(eng, lambda e: e.memset(out.ap, val), writes=out.tks)

    def rsqrt_act(self, out, in_, bias):
        self.act(out, in_, AF.Ln, bias=bias, scale=1.0)
        self.act(out, out, AF.Exp, scale=-0.5)

    def recip_act(self, out, in_):
        self.act(out, in_, AF.Ln)
        self.act(out, out, AF.Exp, scale=-1.0)

    def recip(self, out, in_):
        self.P.op("dve", lambda e: e.reciprocal(out=out.ap, in_=in_.ap), reads=in_.tks, writes=out.tks)

    def dma(self, q, chan, out, in_, slow=False):
        if slow:
            return self.P.dma(q, chan, lambda e: e.dma_start(out=out.ap, in_=in_.ap, allow_slow_non_contiguous=True),
                              reads=in_.tks, writes=out.tks)
        return self.P.dma(q, chan, lambda e: e.dma_start(out=out.ap, in_=in_.ap), reads=in_.tks, writes=out.tks)

    def ps(self, fam):
        r = self.psfam[fam]
        v = r[0][r[1] % len(r[0])]
        r[1] += 1
        return v

    def wload(self, src_ap, npart, dims):
        i = self.wi % NW
        self.wi += 1
        slot = self.wslots[i]
        n = dims[0] * dims[1]
        view = V(slot.ap[0:npart, 0:n].rearrange("p (a b) -> p a b", b=dims[1]), slot.tks)
        self.dma("pool", f"w{i}", view, V(src_ap, []))
        return view

    def build(self):
        nc = self.nc
        st = self.st
        B_ = self
        x = self.dram_in("x", [S, D])
        mem = self.dram_in("mem", [256, D])
        pos = self.dram_in("pos", [1, S], I32)
        W = {}
        for name, shape in [
            ("ffn1_norm_g", [2, D]), ("ffn1_w_gate", [2, D, DFF]), ("ffn1_w_up", [2, D, DFF]), ("ffn1_w_down", [2, DFF, D]),
            ("mix_norm_g", [2, D]), ("mem_norm_g", [2, D]), ("w_mem_kv", [2, D, 512]),
            ("mem_q_norm_g", [2, HD]), ("mem_k_norm_g", [2, HD]), ("w_o", [2, D, D]),
            ("ffn2_norm_g", [2, D]), ("ffn2_w_gate", [2, D, DFF]), ("ffn2_w_up", [2, D, DFF]), ("ffn2_w_down", [2, DFF, D]),
            ("a_w_in", [1, D, 1792]), ("a_dw_kernel", [1, CONVW, 1, 768]), ("a_dw_bias", [1, 768]),
            ("a_ln_g", [1, 768]), ("a_ln_b", [1, 768]), ("kv_norm_g", [D]), ("w_kv", [D, 1536]),
            ("k_norm_g", [HD]), ("b_w_in", [1, D, D]), ("b_q_norm_g", [1, HD]),
        ]:
            W[name] = self.dram_in(name, shape)
        c_ident = self.dram_in("c_ident", [128, 128])
        c_ones = self.dram_in("c_ones", [128, 128])
        c_tri = self.dram_in("c_tri", [128, 128])
        c_neg = self.dram_in("c_neg", [128, 128])
        c_perm = self.dram_in("c_perm", [128, 128])
        c_onesblk = self.dram_in("c_onesblk", [128, 128])
        c_rope = self.dram_in("c_rope", [128, 2])
        c_ind = self.dram_in("c_ind", [16, S])
        out = nc.dram_tensor("out", [S, D], F32, kind="ExternalOutput").ap()
        kscr_t = nc.dram_tensor("kscr", [NH_B, HD, S], BF16, kind="Internal").ap()
        vscr_t = nc.dram_tensor("vscr", [S, NH_B * HD], BF16, kind="Internal").ap()
        kscr = [V(kscr_t[h], [Tk()]) for h in range(NH_B)]
        vscr_tk = [Tk() for _ in range(NT)]

        psb = [st.enter_context(nc.psum_tensor(f"psb{i}", [128, 512], F32)) for i in range(7)]
        pst = st.enter_context(nc.psum_tensor("pst", [128, 1024], BF16))
        pv = [V(p[:], [Tk(excl=True)]) for p in psb]
        self.psfam = {"A": [pv[0:2], 0], "B": [pv[2:4], 0], "S": [pv[4:6], 0], "M": [pv[6:7], 0],
                      "T": [[V(pst[:], [Tk(excl=True)])], 0]}

        hT, hT_all = self.sbc("hT", [128, 8, T], F32)
        xn, xn_all = self.sbc("xn", [128, 8, T], BF16)
        rbc = self.sb("rbc", [128, T], F32)
        self.ring("sq", 4, [128, T], BF16)
        self.ring("sgt", 2, [128, T], BF16)
        self.ring("sgf", 2, [128, T], F32)
        self.wslots = [self.sb(f"wslot{i}", [128, 1024], BF16) for i in range(NW)]
        self.wi = 0
        hglu, hglu_all = self.sbc("hglu", [128, 6, HALO + T], BF16)
        for nm, dt in [("qf", F32), ("rr", F32), ("t2", F32)]:
            self.ring(nm, 4, [128, T], dt)
        self.ring("sqh", 4, [128, T], BF16)
        self.ring("qpb", 2, [128, T], BF16)
        qm, qm_all = self.sbc("qm", [64, NH_M, T], BF16)
        catm, catm_all = self.sbc("catm", [128, NH_M // 2, T], BF16)
        self.ring("ctmp", 2, [64, T], BF16)
        shup_b = self.sb("shup_b", [64, 128], BF16)
        memK = [self.sbc(f"memK{l}", [64, NH_M, 256], BF16)[0] for l in range(2)]
        memV = [self.sb(f"memV{l}", [128, 2, 256], BF16) for l in range(2)]
        self.ring("pT", 4, [128, T], BF16)
        self.ring("rec", 2, [64, T], F32)
        ropeC = self.sb("ropeC", [128, T], F32)
        ropeS = self.sb("ropeS", [128, T], F32)
        posi = self.sb("posi", [128, T], I32)
        rta = self.sb("rta", [128, T], F32)
        rtb = self.sb("rtb", [128, T], F32)
        kmeanT = self.sb("kmeanT", [128, NH_B // 2, 16], F32)
        km2 = self.sb("km2", [128, 2], F32)
        top8, top8_all = self.sbc("top8", [128, NH_B, 8], F32)
        ident_f = self.sb("ident_f", [128, 128], F32)
        ident_b = self.sb("ident_b", [128, 128], BF16)
        ones_b = self.sb("ones_b", [128, 128], BF16)
        tri_b = self.sb("tri_b", [128, 128], BF16)
        neg_b = self.sb("neg_b", [128, 128], BF16)
        perm_f = self.sb("perm_f", [128, 128], F32)
        onesblk_b = self.sb("onesblk_b", [128, 128], BF16)
        ones_f = self.sb("ones_f", [128, 64], F32)
        self.ring("r1", 2, [128, T], F32)
        ropec = self.sb("ropec", [128, 2], F32)
        gall = self.sb("gall", [128, 7, 8], F32)
        gmem = self.sb("gmem", [128, 2, 8], F32)
        hg = self.sb("hg", [128, 8], F32)
        dwk = self.sb("dwk", [128, 6, CONVW], F32)
        cvp = self.sb("cvp", [128, 3, 6], F32)

        USZ = 28160
        Ut = st.enter_context(nc.sbuf_tensor("U", [128, USZ], BF16))
        ucur = {"tks": []}

        def uview(off_b, nbytes, dt, npart=128, p0=0):
            a = Ut[p0:p0 + npart, off_b // 2:(off_b + nbytes) // 2]
            if dt == F32:
                a = a.bitcast(F32)
            elif dt == I32:
                a = a.bitcast(I32)
            return a

        def uswitch(views):
            tks = []
            for v in views:
                for t in v.tks:
                    if t not in tks:
                        tks.append(t)
            alias_switch(ucur["tks"], tks)
            ucur["tks"] = tks

        h1_tks = [Tk() for _ in range(NF)]
        h1a = uview(0, NF * T * 2, BF16).rearrange("p (f t) -> p f t", t=T)
        h1 = [V(h1a[:, f], [h1_tks[f]]) for f in range(NF)]
        xs = [V(uview(i * 4096, 4096, F32), [Tk()]) for i in range(2)]
        acc_tks = [Tk() for _ in range(6)]
        acca = uview(0, 6 * T * 4, F32).rearrange("p (c t) -> p c t", t=T)
        acc = [V(acca[:, c], [acc_tks[c]]) for c in range(6)]
        prim_tks = [Tk() for _ in range(6)]
        prima = uview(12288, 6 * T * 2, BF16).rearrange("p (c t) -> p c t", t=T)
        prim = [V(prima[:, c], [prim_tks[c]]) for c in range(6)]
        stat = [V(uview(18432 + i * 2048, 2048, F32), [Tk()]) for i in range(4)]
        accb_ring = [V(uview(26624 + i * 1024, 1024, BF16), [Tk()]) for i in range(2)]
        lnt_ring = [V(uview(28672 + i * 2048, 2048, F32), [Tk()]) for i in range(2)]
        dg_ring = [V(uview(32768 + i * 256, 256, BF16), [Tk()]) for i in range(8)]
        mixA_views = acc + prim + stat + accb_ring + lnt_ring + dg_ring
        vtok_tks = [Tk() for _ in range(4)]
        vtoka = uview(0, 4 * 768 * 2, BF16).rearrange("p (a b) -> p a b", b=768)
        vtok = [V(vtoka[:, i], [vtok_tks[i]]) for i in range(4)]
        kb_ring = [V(uview(6144 + i * 1024, 1024, BF16), [Tk()]) for i in range(2)]
        kv_views = vtok + kb_ring
        qa_tks = [Tk() for _ in range(NH_B)]
        qaug_a = uview(0, NH_B * T * 2, BF16, npart=80).rearrange("p (h t) -> p h t", t=T)
        qaug = [V(qaug_a[:, h], [qa_tks[h]]) for h in range(NH_B)]
        qaug_all = V(qaug_a, qa_tks)
        co_tks = [Tk() for _ in range(NH_B // 2)]
        cato_a = uview(12288, (NH_B // 2) * T * 2, BF16).rearrange("p (h t) -> p h t", t=T)
        cato = [V(cato_a[:, n], [co_tks[n]]) for n in range(NH_B // 2)]
        G = V(uview(24576, 4 * NH_B * 16 * 4, F32).rearrange("p (q h n) -> p q h n", h=NH_B, n=16), [Tk()])
        Mb = []
        Mb_heads = []
        for i in range(2):
            a_ = uview(27648 + i * 1920, 1920, BF16).rearrange("p (h n) -> p h n", n=80)
            tks_ = [Tk() for _ in range(NH_B)]
            Mb.append(V(a_, tks_))
            Mb_heads.append([V(a_[:, h, :], [tks_[h]]) for h in range(NH_B)])
        Kt = [V(uview(31488 + i * 8192, 8192, BF16, npart=80), [Tk()]) for i in range(2)]
        Vt = [V(uview(47872 + i * 4160, 4160, BF16).rearrange("p (k d) -> p k d", d=HD + 1), [Tk()]) for i in range(2)]
        mixB_views = qaug + cato + [G] + Mb + Kt + Vt
        memf = V(uview(0, 8192, F32).rearrange("p (m f) -> p m f", f=D), [Tk()])
        memn = V(uview(8192, 4096, BF16).rearrange("p (m f) -> p m f", f=D), [Tk()])
        memT_tks = [Tk() for _ in range(8)]
        memTa = uview(12288, 4096, BF16).rearrange("p (c m) -> p c m", m=256)
        memT = [V(memTa[:, c], [memT_tks[c]]) for c in range(8)]
        mss = V(uview(16384, 8, F32), [Tk()])
        junk = V(uview(16640, 2048, BF16), [Tk()])
        init_views = [memf, memn, mss, junk] + memT

        sp_toks = []

        def ld(dst, src_ap, q="sp", chan="c", slow=False):
            return self.dma(q, chan, dst, V(src_ap, []), slow=slow)

        ld(ident_f, c_ident[:, :])
        ld(ropec, c_rope[:, :])
        gst = self.sb("gst", [56, 128], F32)
        for i, nm in enumerate(["ffn1_norm_g", "mix_norm_g", "ffn2_norm_g"]):
            ld(gst[16 * i:16 * i + 16, :], W[nm].rearrange("l (c p) -> (l c) p", p=128))
        ld(gst[48:56, :], W["kv_norm_g"].rearrange("(c p) -> c p", p=128))
        ld(ones_b, c_ones[:, :], q="pool", chan="cb")
        for e_ in ("pe", "act", "dve", "pool"):
            self.P.wait_all(e_, [("c_c", self.P.cnt["c_c"]), ("c_cb", self.P.cnt["c_cb"])])
        pgt = self.ps("M")
        self.tr(pgt[:, 0:56], gst, ident_f[0:56, 0:56])
        self.ts(gall, pgt[:, 0:56].rearrange("p (i c) -> p i c", c=8), 32.0)
        ld(perm_f, c_perm[:, :], chan="c2")
        ld(ident_b, c_ident[:, :], q="pool", chan="cb2")
        ld(tri_b, c_tri[:, :], q="pool", chan="cb2")
        ld(neg_b, c_neg[:, :], q="pool", chan="cb2")
        ld(onesblk_b, c_onesblk[:, :], q="pool", chan="cb2")
        ld(shup_b, c_ident[64:128, :], q="pool", chan="cb2")
        for l in range(2):
            ld(gmem[:, l, :], W["mem_norm_g"][l].rearrange("(c p) -> p c", p=128), chan="c2", slow=True)
        hsrc = [W["mem_q_norm_g"][0], W["mem_q_norm_g"][1], W["mem_k_norm_g"][0], W["mem_k_norm_g"][1],
                W["k_norm_g"], W["b_q_norm_g"][0]]
        for i, g in enumerate(hsrc):
            ld(hg[0:64, i:i + 1], g.rearrange("(p o) -> p o", o=1), chan="c2")
            ld(hg[64:128, i:i + 1], g.rearrange("(p o) -> p o", o=1), chan="c2")
        for c in range(6):
            ld(dwk[:, c, :], W["a_dw_kernel"][0, :, 0, c * 128:(c + 1) * 128].rearrange("j p -> p j"), chan="c2", slow=True)
        for i, nm in enumerate(["a_dw_bias", "a_ln_g", "a_ln_b"]):
            ld(cvp[:, i, :], W[nm][0].rearrange("(c p) -> p c", p=128), chan="c2", slow=True)
        for c in range(6):
            self.memset(hglu[c][:, 0:HALO], 0.0)
        self.memset(kmeanT, 0.0)
        self.memset(ones_f, 1.0)

        def late_params():
            for e_ in ("pe", "act", "dve", "pool"):
                self.P.wait_all(e_, [("c_c2", self.P.cnt["c_c2"]), ("c_cb2", self.P.cnt["c_cb2"])])
            self.ts(hg[:, 2:5], hg[:, 2:5], 8.0)

        def norm(gi):
            psn = self.ps("M")
            for c in range(8):
                sq = self.nxt("sq")
                if c % 2 == 0:
                    self.act(sq, hT[c], AF.Square)
                else:
                    self.tt(sq, hT[c], hT[c], ALU.mult)
                self.mm(psn, ones_b, sq, start=(c == 0), stop=(c == 7))
            self.rsqrt_act(rbc, psn, float(D * EPS))
            for c in range(8):
                self.stt(xn[c], hT[c], gall[:, gi, c:c + 1], rbc, ALU.mult, ALU.mult)

        def proj(ps_out, w, lo, hi):
            for kc in range(8):
                self.mm(ps_out, w[:, kc, lo:hi], xn[kc], start=(kc == 0), stop=(kc == 7))

        def wchunk(wap, n):
            return self.wload(wap[:, n * 128:(n + 1) * 128].rearrange("(kc p) c -> p kc c", p=128), 128, (8, 128))

        def ffn(l, which, gi):
            uswitch(h1)
            norm(gi)
            wg, wu, wd = W[f"{which}_w_gate"][l], W[f"{which}_w_up"][l], W[f"{which}_w_down"][l]
            for f in range(NF):
                a = wchunk(wg, f)
                b = wchunk(wu, f)
                pg = self.ps("A")
                pu = self.ps("B")
                proj(pg, a, 0, 128)
                proj(pu, b, 0, 128)
                sg = self.nxt("sgt")
                self.act(sg, pg, AF.Silu)
                self.tt(h1[f], pu, sg, ALU.mult)
            for c in range(8):
                po = self.ps("S")
                f0 = 0
                for nf in (8, 8, 6):
                    w = self.wload(wd[f0 * 128:(f0 + nf) * 128, c * 128:(c + 1) * 128].rearrange("(f p) c -> p f c", p=128),
                                   128, (nf, 128))
                    for i in range(nf):
                        f = f0 + i
                        self.mm(po, w[:, i, :], h1[f], start=(f == 0), stop=(f == NF - 1))
                    f0 += nf
                self.stt(hT[c], po, 0.5, hT[c], ALU.mult, ALU.add)

        def pairs_batch(items):
            n = len(items)
            banks = [self.ps("A"), self.ps("A"), self.ps("B"), self.ps("B")][:n]
            qf = [self.nxt("qf") for _ in range(n)]
            sqh = [self.nxt("sqh") for _ in range(n)]
            rr = [self.nxt("rr") for _ in range(n)]
            t2 = [self.nxt("t2") for _ in range(n)]
            for i, it in enumerate(items):
                proj(banks[i], it[0], 0, 128)
            for i in range(n):
                self.copy(qf[i], banks[i], eng=("act" if i % 2 else "dve"))
            for i in range(n):
                self.tt(sqh[i], qf[i], qf[i], ALU.mult)
            for i in range(n):
                self.mm(banks[i], onesblk_b, sqh[i])
            for i in range(n):
                self.rsqrt_act(rr[i], banks[i], float(HD * EPS))
            for i, it in enumerate(items):
                self.stt(qf[i], qf[i], hg[:, it[1]:it[1] + 1], rr[i], ALU.mult, ALU.mult)
            rp = [i for i, it in enumerate(items) if it[2]]
            for i in rp:
                self.mm(banks[i], perm_f, qf[i])
            for i in rp:
                self.tt(t2[i], banks[i], ropeS, ALU.mult)
            for i in rp:
                self.tt(qf[i], qf[i], ropeC, ALU.mult)
            for i in rp:
                self.tt(qf[i], qf[i], t2[i], ALU.add)
            for i, it in enumerate(items):
                it[3](qf[i])

        def split_pair(q, dst_even, dst_odd):
            self.copy(dst_even, q[0:64, :], eng="act")
            qpb = self.nxt("qpb")
            self.copy(qpb[64:128, :], q[64:128, :])
            psh = self.ps("S")
            self.mm(psh[0:64, :], ident_b[64:128, 64:128], qpb[64:128, :])
            self.copy(dst_odd, psh[0:64, :], eng="act")

        def rope_tables(ti):
            ld(posi, pos[0:1, ti * T:(ti + 1) * T].partition_broadcast(128), chan="pos")
            self.copy(rta, posi)
            self.ts(rta, rta, ropec[:, 0:1])
            self.ts(rtb, rta, MAG, None, ALU.add)
            self.ts(rtb, rtb, -MAG, None, ALU.add)
            self.tt(rta, rta, rtb, ALU.subtract)
            self.act(ropeS, rta, AF.Sin, scale=6.28318)
            self.ts(ropeS, ropeS, ropec[:, 1:2])
            self.ts(rta, rta, 0.25, None, ALU.add)
            self.ts(rtb, rta, 0.5, None, ALU.is_gt)
            self.tt(rta, rta, rtb, ALU.subtract)
            self.act(ropeC, rta, AF.Sin, scale=6.28318)

        def put_head(dst_pair, h, po64, rec):
            if h % 2 == 0:
                self.tt(dst_pair[0:64, :], po64, rec, ALU.mult)
            else:
                ct = self.nxt("ctmp")
                self.tt(ct, po64, rec, ALU.mult)
                psh = self.ps("M")
                self.mm(psh, shup_b, ct)
                self.copy(dst_pair[64:128, :], psh[64:128, :], eng="act")

        def mem_attn(l):
            for h in range(NH_M):
                po = self.ps("A")
                pd = self.ps("B")
                for mt in range(2):
                    pss = self.ps("S")
                    self.mm(pss, memK[l][h][:, mt * 128:(mt + 1) * 128], qm[h])
                    pT = self.nxt("pT")
                    self.act(pT, pss, AF.Exp)
                    self.mm(po[0:64, :], memV[l][:, mt, h * 64:(h + 1) * 64], pT, start=(mt == 0), stop=(mt == 1))
                    self.mm(pd[0:64, :], ones_b[:, 0:64], pT, start=(mt == 0), stop=(mt == 1))
                rec = self.nxt("rec")
                self.recip_act(rec, pd[0:64, :])
                put_head(catm[h // 2], h, po[0:64, :], rec)

        def mem_q_items(wap, n0, gcol):
            return [(wchunk(wap, n0 + i), gcol, False,
                     (lambda q, i=i: split_pair(q, qm[2 * i], qm[2 * i + 1]))) for i in range(2)]

        def wo_proj(l, groups):
            wo = W["w_o"][l]
            for c in range(8):
                po = self.ps("S")
                n_mm = sum(len(g[3]) for g in groups)
                k = 0
                for (r0, npart, ng, rhs_list) in groups:
                    w = self.wload(wo[r0:r0 + npart * ng, c * 128:(c + 1) * 128].rearrange("(g p) c -> p g c", p=npart),
                                   npart, (ng, 128))
                    for gi_, r in enumerate(rhs_list):
                        self.mm(po, w[:, gi_, :], r, start=(k == 0), stop=(k == n_mm - 1))
                        k += 1
                self.tt(hT[c], po, hT[c], ALU.add)

        def mixer_a():
            uswitch(mixA_views)
            norm(2)
            win = W["a_w_in"][0]
            for c in range(6):
                wa = wchunk(win, c)
                wg = wchunk(win, 6 + c)
                pa = self.ps("A")
                pg = self.ps("B")
                proj(pa, wa, 0, 128)
                proj(pg, wg, 0, 128)
                sg = self.nxt("sgf")
                self.act(sg, pg, AF.Sigmoid)
                self.tt(hglu[c][:, HALO:HALO + T], pa, sg, ALU.mult)
            pairs_batch(mem_q_items(win, 12, 0))
            k_ = 0
            for c in range(6):
                pc = self.ps("A")
                for j in range(CONVW):
                    dg = dg_ring[k_ % 8]
                    k_ += 1
                    self.ts(dg, ident_f, dwk[:, c, j:j + 1])
                    self.mm(pc, dg, hglu[c][:, j:j + T], start=(j == 0), stop=(j == CONVW - 1))
                self.act(acc[c], pc, AF.Identity, bias=cvp[:, 0, c:c + 1])
            for c in range(6):
                self.copy(hglu[c][:, 0:HALO], hglu[c][:, T:T + HALO], eng="act")
            mem_attn(0)
            p1 = self.ps("A")
            p2 = self.ps("B")
            for c in range(6):
                ab = accb_ring[c % 2]
                sq = self.nxt("sq")
                self.copy(ab, acc[c], eng="act")
                self.act(sq, acc[c], AF.Square)
                self.mm(p1, ones_b, ab, start=(c == 0), stop=(c == 5))
                self.mm(p2, ones_b, sq, start=(c == 0), stop=(c == 5))
            mean, msq, rs, mrs = stat
            self.act(mean, p1, AF.Copy, scale=1.0 / 768)
            self.tt(msq, mean, mean, ALU.mult)
            self.stt(msq, p2, 1.0 / 768, msq, ALU.mult, ALU.subtract)
            self.rsqrt_act(rs, msq, float(EPS))
            self.tt(mrs, mean, rs, ALU.mult)
            for c in range(6):
                lt = lnt_ring[c % 2]
                self.tt(lt, acc[c], rs, ALU.mult)
                self.tt(lt, lt, mrs, ALU.subtract)
                self.act(prim[c], lt, AF.Silu, bias=cvp[:, 2, c:c + 1], scale=cvp[:, 1, c:c + 1])
            wo_proj(0, [(0, 128, 6, prim), (768, 128, 2, catm)])

        def kv_stage(ti):
            uswitch(kv_views)
            norm(6)
            wkv = W["w_kv"]
            for n0 in range(0, NH_B // 2, 3):
                ws = [wchunk(wkv, n0 + i) for i in range(3)]

                def k_consume(kr, n):
                    kb = kb_ring[n % 2]
                    self.copy(kb, kr, eng="act")
                    dst = V(kscr_t[2 * n:2 * n + 2].rearrange("h d s -> (h d) s")[:, ti * T:(ti + 1) * T],
                            kscr[2 * n].tks + kscr[2 * n + 1].tks)
                    self.dma("sp", f"ks{n}", dst, kb)
                    self.P.op("dve", lambda e, kr=kr: e.tensor_reduce(
                        out=km2.ap, in_=kr.ap.rearrange("p (n j) -> p n j", j=256), axis=AX.X, op=ALU.add),
                        reads=kr.tks, writes=km2.tks)
                    self.ts(kmeanT[:, n, 2 * ti:2 * ti + 2], km2, 1.0 / 256)
                pairs_batch([(ws[i], 4, True, (lambda q, n=n0 + i: k_consume(q, n))) for i in range(3)])
            wv = [self.wload(wkv[kc * 128:(kc + 1) * 128, 768:1536].rearrange("p (o c) -> p o c", o=1), 128, (1, 768))
                  for kc in range(8)]
            for tt_ in range(4):
                p1 = self.ps("A")
                p2 = self.ps("B")
                for kc in range(8):
                    lhs = xn[kc][:, tt_ * 128:(tt_ + 1) * 128]
                    self.mm(p1, lhs, wv[kc][:, 0, 0:512], start=(kc == 0), stop=(kc == 7))
                    self.mm(p2[:, 0:256], lhs, wv[kc][:, 0, 512:768], start=(kc == 0), stop=(kc == 7))
                self.copy(vtok[tt_][:, 0:512], p1, eng="act")
                self.copy(vtok[tt_][:, 512:768], p2[:, 0:256])
                r0 = ti * T + tt_ * 128
                self.dma("sp", f"vs{tt_}", V(vscr_t[r0:r0 + 128, :], [vscr_tk[ti]]), vtok[tt_])

        def mixer_b(ti):
            uswitch(mixB_views)
            norm(3)
            win = W["b_w_in"][0]
            gating = ti >= 2
            for i in range(2):
                ld(Kt[i][64:80, :], c_ind[:, :], q="pool", chan=f"ind{i}")
                self.memset(Vt[i][:, :, HD:HD + 1], 1.0)
            if not gating:
                self.memset(qaug_all[64:80], 0.0)
            mem_q_items_b = [None, None]
            for n0 in range(0, NH_B // 2, 3):
                ws = [wchunk(win, n0 + i) for i in range(3)]
                j_ = n0 // 3
                mem_q_items_b[j_] = (wchunk(win, 6 + j_), 1, False,
                                     (lambda q, j_=j_: split_pair(q, qm[2 * j_], qm[2 * j_ + 1])))

                def q_consume(qr, n):
                    split_pair(qr, qaug[2 * n][0:64, :], qaug[2 * n + 1][0:64, :])
                    if gating:
                        for par in range(2):
                            h = 2 * n + par
                            pg = self.ps("S")
                            for qt in range(4):
                                self.mm(pg[:, qt * 16:(qt + 1) * 16],
                                        qr[par * 64:(par + 1) * 64, qt * 128:(qt + 1) * 128],
                                        kmeanT[par * 64:(par + 1) * 64, n, :])
                            self.copy(G[:, :, h, :], pg[:, 0:64].rearrange("p (q n) -> p q n", n=16))
                mq = mem_q_items_b[n0 // 3]
                pairs_batch([(ws[i], 5, True, (lambda q, n=n0 + i: q_consume(q, n))) for i in range(3)] + [mq])
            if gating:
                for qt in range(4):
                    own = 2 * ti + qt // 2
                    self.memset(G[:, qt, :, own:16], -BIG)
                    mb = Mb[qt % 2]
                    mbh = Mb_heads[qt % 2]
                    for h in range(NH_B):
                        self.P.op("dve", lambda e, qt=qt, h=h: e.max(out=top8[h].ap, in_=G.ap[:, qt, h, :]),
                                  reads=G.tks, writes=top8[h].tks)
                    thr = V(top8_all.ap[:, :, 2:3].to_broadcast([128, NH_B, 16]), top8_all.tks)
                    self.tt(mb[:, :, 64:80], G[:, qt], thr, ALU.is_lt)
                    self.ts(mb[:, :, 64:80], mb[:, :, 64:80], -BIG)
                    self.memset(mb[:, :, 64 + own:65 + own], 0.0)
                    for h0 in range(0, NH_B, 4):
                        pt = self.ps("T")
                        for hh in range(4):
                            self.tr(pt[0:80, hh * 128:(hh + 1) * 128], mbh[h0 + hh], ident_b)
                        self.copy(qaug_all[64:80, h0:h0 + 4, qt * 128:(qt + 1) * 128],
                                  pt[64:80, 0:512].rearrange("p (h q) -> p h q", q=128))
            nkeys = (ti + 1) * T
            pending = []
            s3 = [self.psfam["S"][0][0], self.psfam["S"][0][1], self.psfam["B"][0][1]]
            s3i = [0]

            def nxt_s():
                v = s3[s3i[0] % 3]
                s3i[0] += 1
                return v

            def finalize(po, h):
                r1 = self.nxt("r1")
                self.recip_act(r1[64:65, :], po[64:65, :])
                pb = self.psfam["B"][0][0]
                self.mm(pb[0:64, :], ones_f[64:65, 0:64], r1[64:65, :])
                rec = self.nxt("rec")
                self.copy(rec, pb[0:64, :], eng="act")
                put_head(cato[h // 2], h, po[0:64, :], rec)

            npast = 4 * ti
            for h in range(NH_B):
                kt = Kt[h % 2]
                vt = Vt[h % 2]
                self.dma("sp", f"kt{h % 2}", kt[0:64, 0:nkeys], kscr[h][:, 0:nkeys])
                self.dma("sp", f"vt{h % 2}", vt[:, 0:nkeys // 128, 0:HD],
                         V(vscr_t[0:nkeys, h * HD:(h + 1) * HD].rearrange("(k p) d -> p k d", p=128),
                           vscr_tk[0:ti + 1]))
                po = self.ps("A")[0:HD + 1, :]
                items = []
                for j in range(npast):
                    def e_qk(pss, j=j):
                        self.mm(pss, kt[0:80, j * 128:(j + 1) * 128], qaug[h])
                    items.append((e_qk, [(j, slice(0, 512), slice(0, 512))]))
                for qb in range(2):
                    i = 2 * ti + qb
                    q0 = qb * 256
                    for n in range(2 * ti, i + 1):
                        def e_qk(pss, n=n, i=i, q0=q0):
                            for kk in range(2):
                                j = 2 * n + kk
                                ksl = kt[0:80, j * 128:(j + 1) * 128]
                                o = pss[:, kk * 256:(kk + 1) * 256]
                                if n < i:
                                    self.mm(o, ksl, qaug[h][:, q0:q0 + 256])
                                elif kk == 0:
                                    self.mm(o[:, 0:128], ksl, qaug[h][:, q0:q0 + 128], start=True, stop=False)
                                    self.mm(o[:, 0:128], ident_b, tri_b, start=False, stop=True)
                                    self.mm(o[:, 128:256], ksl, qaug[h][:, q0 + 128:q0 + 256])
                                else:
                                    self.mm(o[:, 0:128], ident_b, neg_b)
                                    self.mm(o[:, 128:256], ksl, qaug[h][:, q0 + 128:q0 + 256], start=True, stop=False)
                                    self.mm(o[:, 128:256], ident_b, tri_b, start=False, stop=True)
                        items.append((e_qk, [(2 * n + kk, slice(kk * 256, (kk + 1) * 256), slice(q0, q0 + 256))
                                             for kk in range(2)]))
                nit = len(items)
                npv = sum(len(it[1]) for it in items)
                pss_q = []
                for a in range(min(2, nit)):
                    p_ = nxt_s()
                    items[a][0](p_)
                    pss_q.append(p_)
                kpv = 0
                for a in range(nit):
                    if a + 2 < nit:
                        p_ = nxt_s()
                        items[a + 2][0](p_)
                        pss_q.append(p_)
                    cur = pss_q[a]
                    pT = self.nxt("pT")
                    self.act(pT, cur, AF.Exp)
                    for (j, psl, osl) in items[a][1]:
                        self.mm(po[:, osl], vt[:, j, :], pT[:, psl], start=(kpv == 0), stop=(kpv == npv - 1))
                        kpv += 1
                    if a == 0 and pending:
                        finalize(*pending.pop())
                pending.append((po, h))
            while pending:
                finalize(*pending.pop())
            mem_attn(1)
            wo_proj(1, [(0, 128, 8, cato + catm)])

        def load_tile(ti):
            uswitch(xs)
            for tt_ in range(4):
                s = xs[tt_ % 2]
                r0 = ti * T + tt_ * 128
                ld(s, x[r0:r0 + 128, :], chan=f"xs{tt_ % 2}")
                for half in range(2):
                    pt = self.ps("A")
                    for cc in range(4):
                        c = half * 4 + cc
                        self.tr(pt[:, cc * 128:(cc + 1) * 128], s[:, c * 128:(c + 1) * 128], ident_f)
                    for cc in range(4):
                        c = half * 4 + cc
                        self.copy(hT[c][:, tt_ * 128:(tt_ + 1) * 128], pt[:, cc * 128:(cc + 1) * 128],
                                  eng=("act" if cc % 2 else "dve"))

        def store_tile(ti):
            uswitch(xs)
            for tt_ in range(4):
                s = xs[tt_ % 2]
                for half in range(2):
                    pt = self.ps("A")
                    for cc in range(4):
                        c = half * 4 + cc
                        self.tr(pt[:, cc * 128:(cc + 1) * 128], hT[c][:, tt_ * 128:(tt_ + 1) * 128], ident_f)
                    self.copy(s[:, half * 512:(half + 1) * 512], pt, eng=("act" if half else "dve"))
                r0 = ti * T + tt_ * 128
                sp_toks.append(self.dma("sp", f"os{tt_ % 2}", V(out[r0:r0 + 128, :], []), s))

        def mem_kv_init(l):
            uswitch(init_views)
            for mt in range(2):
                ld(memf[:, mt, :], mem[mt * 128:(mt + 1) * 128, :], chan="mem")
            for mt in range(2):
                self.act(junk, memf[:, mt, :], AF.Square, accum=mss[:, mt:mt + 1])
            self.act(mss, mss, AF.Sqrt, bias=float(EPS), scale=1.0 / D)
            self.recip(mss, mss)
            for mt in range(2):
                self.ts(memn[:, mt, :], memf[:, mt, :], mss[:, mt:mt + 1])
            for mt in range(2):
                for c0 in (0, 4):
                    pt = self.ps("T")
                    for cc in range(4):
                        c = c0 + cc
                        self.tr(pt[:, cc * 128:(cc + 1) * 128], memn[:, mt, c * 128:(c + 1) * 128], ident_b)
                    for cc in range(4):
                        c = c0 + cc
                        self.ts(memT[c][:, mt * 128:(mt + 1) * 128], pt[:, cc * 128:(cc + 1) * 128], gmem[:, l, c:c + 1])
            wm = W["w_mem_kv"][l]
            for h in range(NH_M):
                if h % 2 == 0:
                    w = wchunk(wm, h // 2)
                pk = self.ps("A")
                for kc in range(8):
                    self.mm(pk[0:64, 0:256], w[:, kc, (h % 2) * 64:(h % 2) * 64 + 64], memT[kc],
                            start=(kc == 0), stop=(kc == 7))
                qf = self.nxt("qf")[0:64, 0:256]
                sqh = self.nxt("sqh")[0:64, 0:256]
                rr = self.nxt("rr")[0:64, 0:256]
                self.copy(qf, pk[0:64, 0:256], eng="act")
                self.tt(sqh, qf, qf, ALU.mult)
                pss = self.ps("M")
                self.mm(pss[0:64, 0:256], ones_b[0:64, 0:64], sqh)
                self.rsqrt_act(rr, pss[0:64, 0:256], float(HD * EPS))
                self.stt(memK[l][h], qf, hg[0:64, 2 + l:3 + l], rr, ALU.mult, ALU.mult)
            wv = [self.wload(wm[kc0 * 128:(kc0 + 4) * 128, 256:512].rearrange("(kc p) c -> p kc c", p=128), 128, (4, 256))
                  for kc0 in (0, 4)]
            for mt in range(2):
                pvv = self.ps("B")
                for kc in range(8):
                    self.mm(pvv[:, 0:256], memT[kc][:, mt * 128:(mt + 1) * 128], wv[kc // 4][:, kc % 4, :],
                            start=(kc == 0), stop=(kc == 7))
                self.copy(memV[l][:, mt, :], pvv[:, 0:256], eng="act")

        stages = ["ffn1_0", "mix_0", "ffn2_0", "kv", "ffn1_1", "mix_1", "ffn2_1"]
        nst = len(stages) if self.stop_after is None else (0 if self.stop_after == 'none' else stages.index(self.stop_after) + 1)
        for ti in range(self.n_tiles):
            load_tile(ti)
            if 'rope' not in _DBG:
                rope_tables(ti)
            for sname in stages[:nst]:
                if sname == "ffn1_0":
                    ffn(0, "ffn1", 0)
                    if ti == 0:
                        late_params()
                        mem_kv_init(0)
                        mem_kv_init(1)
                elif sname == "mix_0":
                    mixer_a()
                elif sname == "ffn2_0":
                    ffn(0, "ffn2", 4)
                elif sname == "kv":
                    kv_stage(ti)
                elif sname == "ffn1_1":
                    ffn(1, "ffn1", 1)
                elif sname == "mix_1":
                    mixer_b(ti)
                elif sname == "ffn2_1":
                    ffn(1, "ffn2", 5)
            store_tile(ti)
        self.P.wait_all("sp", sp_toks)
        self.P.emit(st)
        st.close()
        return nc


def _consts():
    ident = np.eye(128, dtype=np.float32)
    ones = np.ones((128, 128), np.float32)
    k = np.arange(128)[:, None]
    q = np.arange(128)[None, :]
    tri = np.where(k <= q, 0.0, -BIG).astype(np.float32)
    neg = np.full((128, 128), -BIG, np.float32)
    perm = np.zeros((128, 128), np.float32)
    onesblk = np.zeros((128, 128), np.float32)
    rope = np.zeros((128, 2), np.float32)
    inv_freq = 1.0 / (500000.0 ** (np.arange(0, 16, 2, dtype=np.float32) / 16.0))
    for b0 in (0, 64):
        onesblk[b0:b0 + 64, b0:b0 + 64] = 1.0
        for d in range(8):
            perm[b0 + d + 8, b0 + d] = 1.0
            perm[b0 + d, b0 + d + 8] = 1.0
        for d in range(16):
            rope[b0 + d, 0] = np.float32(inv_freq[d % 8]) / np.float32(2.0 * math.pi)
            rope[b0 + d, 1] = -1.0 if d < 8 else 1.0
    ind = np.zeros((16, S), np.float32)
    for n in range(16):
        ind[n, n * 256:(n + 1) * 256] = 1.0
    return {"c_ident": ident, "c_ones": ones, "c_tri": tri, "c_neg": neg, "c_perm": perm, "c_onesblk": onesblk,
            "c_rope": rope, "c_ind": ind}


_WNAMES = ["ffn1_norm_g", "ffn1_w_gate", "ffn1_w_up", "ffn1_w_down", "mix_norm_g", "mem_norm_g", "w_mem_kv",
           "mem_q_norm_g", "mem_k_norm_g", "w_o", "ffn2_norm_g", "ffn2_w_gate", "ffn2_w_up", "ffn2_w_down",
           "a_w_in", "a_dw_kernel", "a_dw_bias", "a_ln_g", "a_ln_b", "kv_norm_g", "w_kv", "k_norm_g",
           "b_w_in", "b_q_norm_g"]


def kernel(stop_after=None, n_cores=8, n_tiles=NT, skip_init=False, **inputs):
    nc = Builder(stop_after=stop_after, n_tiles=n_tiles, skip_init=skip_init).build()
    consts = _consts()
    shared = {n: np.ascontiguousarray(np.asarray(inputs[n], dtype=np.float32)) for n in _WNAMES}
    shared.update(consts)
    x = np.asarray(inputs["x"], dtype=np.float32)
    mem = np.asarray(inputs["mem"], dtype=np.float32)
    pos = np.asarray(inputs["positions"], dtype=np.int32)
    in_maps = []
    for b in range(n_cores):
        m = dict(shared)
        m["x"] = np.ascontiguousarray(x[b])
        m["mem"] = np.ascontiguousarray(mem[b])
        m["pos"] = np.ascontiguousarray(pos[b:b + 1])
        in_maps.append(m)
    res = run_bass_kernel_spmd(nc, in_maps, core_ids=list(range(n_cores)))
    return np.stack([np.asarray(r["out"], dtype=np.float32) for r in res.results], axis=0)
```

```python
import math
from contextlib import ExitStack

import numpy as np
import concourse.bass as bass
import concourse.mybir as mybir
from concourse.bass_utils import run_bass_kernel_spmd

F32 = mybir.dt.float32
BF16 = mybir.dt.bfloat16
I32 = mybir.dt.int32
AF = mybir.ActivationFunctionType
ALU = mybir.AluOpType
AX = mybir.AxisListType

D = 1024
S = 4096
DFF = 2816
NF = DFF // 128
T = 512
NT = S // T
HD = 64
NH_B = 12
NH_M = 4
CONVW = 31
HALO = CONVW - 1
EPS = 1e-6
BIG = 30000.0
MAG = 12582912.0
NW = 10
ENGS = ("pe", "act", "dve", "pool", "sp")
import os as _os
_DBG = _os.environ.get('DBG_SKIP', '')


class Tk:
    __slots__ = ("w", "rs", "excl")

    def __init__(self, excl=False):
        self.w = None
        self.rs = []
        self.excl = excl


class V:
    __slots__ = ("ap", "tks")

    def __init__(self, ap, tks):
        self.ap = ap
        self.tks = tks

    def __getitem__(self, idx):
        return V(self.ap[idx], self.tks)

    def rearrange(self, pat, **kw):
        return V(self.ap.rearrange(pat, **kw), self.tks)


class Prog:
    def __init__(self, nc):
        self.nc = nc
        self.ops = {e: [] for e in ENGS}
        self.cnt = {}
        self.seen = {e: {} for e in ENGS}
        self.semnames = []

    def _key(self, key):
        if key not in self.cnt:
            self.cnt[key] = 0
            self.semnames.append(key)
        return key

    def _deps(self, eng, reads, writes):
        deps = {}

        def add(tok):
            if tok is None:
                return
            k, v = tok
            if deps.get(k, 0) < v:
                deps[k] = v
        for t in reads:
            add(t.w)
            if t.excl:
                for r in t.rs:
                    if r[0] != eng:
                        add(r)
        for t in writes:
            add(t.w)
            for r in t.rs:
                add(r)
        waits = []
        for k, v in deps.items():
            if k == "pe" and eng == "pe":
                continue
            if self.seen[eng].get(k, 0) >= v:
                continue
            self.seen[eng][k] = v
            waits.append((k, v))
        return waits

    def _mark(self, tok, reads, writes):
        for t in reads:
            t.rs.append(tok)
            if len(t.rs) > 64:
                best = {}
                for k, v in t.rs:
                    if best.get(k, 0) < v:
                        best[k] = v
                t.rs = list(best.items())
        for t in writes:
            t.w = tok
            t.rs = []

    def op(self, eng, fn, reads=(), writes=(), inc=True):
        waits = self._deps(eng, reads, writes)
        self._key(eng)
        if inc:
            self.cnt[eng] += 1
            tok = (eng, self.cnt[eng])
        else:
            tok = (eng, self.cnt[eng] + 1)
        self.ops[eng].append((waits, fn, (eng, 1) if inc else None))
        self._mark(tok, reads, writes)
        return tok

    def dma(self, q, chan, fn, reads=(), writes=()):
        waits = self._deps(q, reads, writes)
        key = self._key("c_" + chan)
        self.cnt[key] += 16
        tok = (key, self.cnt[key])
        self.ops[q].append((waits, fn, (key, 16)))
        self._mark(tok, reads, writes)
        return tok

    def wait_all(self, eng, toks):
        waits = []
        for k, v in toks:
            if self.seen[eng].get(k, 0) >= v:
                continue
            self.seen[eng][k] = v
            waits.append((k, v))
        self.ops[eng].append((waits, None, None))

    def emit(self, stack):
        nc = self.nc
        sems = {}
        for key in self.semnames:
            sems[key] = stack.enter_context(nc.semaphore("s_" + key))
        block = stack.enter_context(nc.Block())

        def runner(name):
            def run(eng):
                for waits, fn, inc in self.ops[name]:
                    for k, v in waits:
                        eng.wait_ge(sems[k], v)
                    if fn is not None:
                        ins = fn(eng)
                        if inc is not None:
                            ins.then_inc(sems[inc[0]], inc[1])
            return run

        block.tensor(runner("pe"))
        block.scalar(runner("act"))
        block.vector(runner("dve"))
        block.gpsimd(runner("pool"))
        block.sync(runner("sp"))


def alias_switch(old_tks, new_tks):
    pend = {}
    for t in old_tks:
        toks = list(t.rs)
        if t.w is not None:
            toks.append(t.w)
        for k, v in toks:
            if pend.get(k, 0) < v:
                pend[k] = v
    pl = list(pend.items())
    for t in new_tks:
        t.w = None
        t.rs = list(pl)


class Builder:
    def __init__(self, stop_after=None, n_tiles=NT, skip_init=False):
        self.stop_after = stop_after
        self.n_tiles = n_tiles
        self.skip_init = skip_init
        self.nc = bass.Bass("TRN2", target_bir_lowering=False)
        self.P = Prog(self.nc)
        self.st = ExitStack()
        self.rings = {}

    def dram_in(self, name, shape, dt=F32):
        return self.nc.dram_tensor(name, list(shape), dt, kind="ExternalInput").ap()

    def sb(self, name, shape, dt):
        t = self.st.enter_context(self.nc.sbuf_tensor(name, list(shape), dt))
        return V(t[:], [Tk()])

    def sbc(self, name, shape, dt):
        t = self.st.enter_context(self.nc.sbuf_tensor(name, list(shape), dt))
        tks = [Tk() for _ in range(shape[1])]
        return [V(t[:, i], [tks[i]]) for i in range(shape[1])], V(t[:], tks)

    def ring(self, name, n, shape, dt):
        self.rings[name] = [[self.sb(f"{name}{i}", shape, dt) for i in range(n)], 0]

    def nxt(self, name):
        r = self.rings[name]
        v = r[0][r[1] % len(r[0])]
        r[1] += 1
        return v

    def mm(self, out, lhsT, rhs, start=True, stop=True):
        self.P.op("pe", lambda e: e.matmul(out.ap, lhsT=lhsT.ap, rhs=rhs.ap, start=start, stop=stop),
                  reads=lhsT.tks + rhs.tks, writes=out.tks, inc=True)

    def tr(self, out, in_, ident):
        self.P.op("pe", lambda e: e.transpose(out=out.ap, in_=in_.ap, identity=ident.ap),
                  reads=in_.tks + ident.tks, writes=out.tks)

    def act(self, out, in_, func, bias=None, scale=None, accum=None):
        reads = list(in_.tks)
        kw = {}
        if bias is not None:
            if isinstance(bias, V):
                reads += bias.tks
                kw["bias"] = bias.ap
            else:
                kw["bias"] = float(bias)
        if scale is not None:
            if isinstance(scale, V):
                reads += scale.tks
                kw["scale"] = scale.ap
            else:
                kw["scale"] = float(scale)
        writes = list(out.tks)
        if accum is not None:
            kw["accum_out"] = accum.ap
            writes += accum.tks
        self.P.op("act", lambda e: e.activation(out=out.ap, in_=in_.ap, func=func, **kw),
                  reads=reads, writes=writes)

    def tt(self, out, a, b, op, eng="dve"):
        self.P.op(eng, lambda e: e.tensor_tensor(out=out.ap, in0=a.ap, in1=b.ap, op=op),
                  reads=a.tks + b.tks, writes=out.tks)

    def ts(self, out, a, s1, s2=None, op0=ALU.mult, op1=None, eng="dve"):
        reads = list(a.tks)

        def low(s):
            if isinstance(s, V):
                reads.extend(s.tks)
                return s.ap
            return None if s is None else float(s)
        a1, a2 = low(s1), low(s2)
        if op1 is None:
            self.P.op(eng, lambda e: e.tensor_scalar(out=out.ap, in0=a.ap, scalar1=a1, scalar2=None, op0=op0),
                      reads=reads, writes=out.tks)
        else:
            self.P.op(eng, lambda e: e.tensor_scalar(out=out.ap, in0=a.ap, scalar1=a1, scalar2=a2, op0=op0, op1=op1),
                      reads=reads, writes=out.tks)

    def stt(self, out, a, scalar, b, op0, op1, eng="dve"):
        reads = a.tks + b.tks
        if isinstance(scalar, V):
            reads = reads + scalar.tks
            sc = scalar.ap
        else:
            sc = float(scalar)
        self.P.op(eng, lambda e: e.scalar_tensor_tensor(out=out.ap, in0=a.ap, scalar=sc, in1=b.ap, op0=op0, op1=op1),
                  reads=reads, writes=out.tks)

    def copy(self, out, in_, eng="dve"):
        if eng == "act":
            self.act(out, in_, AF.Copy)
        else:
            self.P.op(eng, lambda e: e.tensor_copy(out=out.ap, in_=in_.ap), reads=in_.tks, writes=out.tks)

    def memset(self, out, val, eng="dve"):
        self.P.op(eng, lambda e: e.memset(out.ap, val), writes=out.tks)

    def rsqrt_act(self, out, in_, bias):
        self.act(out, in_, AF.Ln, bias=bias, scale=1.0)
        self.act(out, out, AF.Exp, scale=-0.5)

    def recip_act(self, out, in_):
        self.act(out, in_, AF.Ln)
        self.act(out, out, AF.Exp, scale=-1.0)

    def recip(self, out, in_):
        self.P.op("dve", lambda e: e.reciprocal(out=out.ap, in_=in_.ap), reads=in_.tks, writes=out.tks)

    def dma(self, q, chan, out, in_, slow=False):
        if slow:
            return self.P.dma(q, chan, lambda e: e.dma_start(out=out.ap, in_=in_.ap, allow_slow_non_contiguous=True),
                              reads=in_.tks, writes=out.tks)
        return self.P.dma(q, chan, lambda e: e.dma_start(out=out.ap, in_=in_.ap), reads=in_.tks, writes=out.tks)

    def ps(self, fam):
        r = self.psfam[fam]
        v = r[0][r[1] % len(r[0])]
        r[1] += 1
        return v

    def wload(self, src_ap, npart, dims):
        i = self.wi % NW
        self.wi += 1
        slot = self.wslots[i]
        n = dims[0] * dims[1]
        view = V(slot.ap[0:npart, 0:n].rearrange("p (a b) -> p a b", b=dims[1]), slot.tks)
        self.dma("pool", f"w{i}", view, V(src_ap, []))
        return view

    def build(self):
        nc = self.nc
        st = self.st
        B_ = self
        x = self.dram_in("x", [S, D])
        mem = self.dram_in("mem", [256, D])
        pos = self.dram_in("pos", [1, S], I32)
        W = {}
        for name, shape in [
            ("ffn1_norm_g", [2, D]), ("ffn1_w_gate", [2, D, DFF]), ("ffn1_w_up", [2, D, DFF]), ("ffn1_w_down", [2, DFF, D]),
            ("mix_norm_g", [2, D]), ("mem_norm_g", [2, D]), ("w_mem_kv", [2, D, 512]),
            ("mem_q_norm_g", [2, HD]), ("mem_k_norm_g", [2, HD]), ("w_o", [2, D, D]),
            ("ffn2_norm_g", [2, D]), ("ffn2_w_gate", [2, D, DFF]), ("ffn2_w_up", [2, D, DFF]), ("ffn2_w_down", [2, DFF, D]),
            ("a_w_in", [1, D, 1792]), ("a_dw_kernel", [1, CONVW, 1, 768]), ("a_dw_bias", [1, 768]),
            ("a_ln_g", [1, 768]), ("a_ln_b", [1, 768]), ("kv_norm_g", [D]), ("w_kv", [D, 1536]),
            ("k_norm_g", [HD]), ("b_w_in", [1, D, D]), ("b_q_norm_g", [1, HD]),
        ]:
            W[name] = self.dram_in(name, shape)
        c_ident = self.dram_in("c_ident", [128, 128])
        c_ones = self.dram_in("c_ones", [128, 128])
        c_tri = self.dram_in("c_tri", [128, 128])
        c_neg = self.dram_in("c_neg", [128, 128])
        c_perm = self.dram_in("c_perm", [128, 128])
        c_onesblk = self.dram_in("c_onesblk", [128, 128])
        c_rope = self.dram_in("c_rope", [128, 2])
        c_ind = self.dram_in("c_ind", [16, S])
        out = nc.dram_tensor("out", [S, D], F32, kind="ExternalOutput").ap()
        kscr_t = nc.dram_tensor("kscr", [NH_B, HD, S], BF16, kind="Internal").ap()
        vscr_t = nc.dram_tensor("vscr", [S, NH_B * HD], BF16, kind="Internal").ap()
        kscr = [V(kscr_t[h], [Tk()]) for h in range(NH_B)]
        vscr_tk = [Tk() for _ in range(NT)]

        psb = [st.enter_context(nc.psum_tensor(f"psb{i}", [128, 512], F32)) for i in range(7)]
        pst = st.enter_context(nc.psum_tensor("pst", [128, 1024], BF16))
        pv = [V(p[:], [Tk(excl=True)]) for p in psb]
        self.psfam = {"A": [pv[0:2], 0], "B": [pv[2:4], 0], "S": [pv[4:6], 0], "M": [pv[6:7], 0],
                      "T": [[V(pst[:], [Tk(excl=True)])], 0]}

        hT, hT_all = self.sbc("hT", [128, 8, T], F32)
        xn, xn_all = self.sbc("xn", [128, 8, T], BF16)
        rbc = self.sb("rbc", [128, T], F32)
        self.ring("sq", 4, [128, T], BF16)
        self.ring("sgt", 2, [128, T], BF16)
        self.ring("sgf", 2, [128, T], F32)
        self.wslots = [self.sb(f"wslot{i}", [128, 1024], BF16) for i in range(NW)]
        self.wi = 0
        hglu, hglu_all = self.sbc("hglu", [128, 6, HALO + T], BF16)
        for nm, dt in [("qf", F32), ("rr", F32), ("t2", F32)]:
            self.ring(nm, 4, [128, T], dt)
        self.ring("sqh", 4, [128, T], BF16)
        self.ring("qpb", 2, [128, T], BF16)
        qm, qm_all = self.sbc("qm", [64, NH_M, T], BF16)
        catm, catm_all = self.sbc("catm", [128, NH_M // 2, T], BF16)
        self.ring("ctmp", 2, [64, T], BF16)
        shup_b = self.sb("shup_b", [64, 128], BF16)
        memK = [self.sbc(f"memK{l}", [64, NH_M, 256], BF16)[0] for l in range(2)]
        memV = [self.sb(f"memV{l}", [128, 2, 256], BF16) for l in range(2)]
        self.ring("pT", 4, [128, T], BF16)
        self.ring("rec", 2, [64, T], F32)
        ropeC = self.sb("ropeC", [128, T], F32)
        ropeS = self.sb("ropeS", [128, T], F32)
        posi = self.sb("posi", [128, T], I32)
        rta = self.sb("rta", [128, T], F32)
        rtb = self.sb("rtb", [128, T], F32)
        kmeanT = self.sb("kmeanT", [128, NH_B // 2, 16], F32)
        km2 = self.sb("km2", [128, 2], F32)
        top8, top8_all = self.sbc("top8", [128, NH_B, 8], F32)
        ident_f = self.sb("ident_f", [128, 128], F32)
        ident_b = self.sb("ident_b", [128, 128], BF16)
        ones_b = self.sb("ones_b", [128, 128], BF16)
        tri_b = self.sb("tri_b", [128, 128], BF16)
        neg_b = self.sb("neg_b", [128, 128], BF16)
        perm_f = self.sb("perm_f", [128, 128], F32)
        onesblk_b = self.sb("onesblk_b", [128, 128], BF16)
        ones_f = self.sb("ones_f", [128, 64], F32)
        self.ring("r1", 2, [128, T], F32)
        ropec = self.sb("ropec", [128, 2], F32)
        gall = self.sb("gall", [128, 7, 8], F32)
        gmem = self.sb("gmem", [128, 2, 8], F32)
        hg = self.sb("hg", [128, 8], F32)
        dwk = self.sb("dwk", [128, 6, CONVW], F32)
        cvp = self.sb("cvp", [128, 3, 6], F32)

        USZ = 28160
        Ut = st.enter_context(nc.sbuf_tensor("U", [128, USZ], BF16))
        ucur = {"tks": []}

        def uview(off_b, nbytes, dt, npart=128, p0=0):
            a = Ut[p0:p0 + npart, off_b // 2:(off_b + nbytes) // 2]
            if dt == F32:
                a = a.bitcast(F32)
            elif dt == I32:
                a = a.bitcast(I32)
            return a

        def uswitch(views):
            tks = []
            for v in views:
                for t in v.tks:
                    if t not in tks:
                        tks.append(t)
            alias_switch(ucur["tks"], tks)
            ucur["tks"] = tks

        h1_tks = [Tk() for _ in range(NF)]
        h1a = uview(0, NF * T * 2, BF16).rearrange("p (f t) -> p f t", t=T)
        h1 = [V(h1a[:, f], [h1_tks[f]]) for f in range(NF)]
        xs = [V(uview(i * 4096, 4096, F32), [Tk()]) for i in range(2)]
        acc_tks = [Tk() for _ in range(6)]
        acca = uview(0, 6 * T * 4, F32).rearrange("p (c t) -> p c t", t=T)
        acc = [V(acca[:, c], [acc_tks[c]]) for c in range(6)]
        prim_tks = [Tk() for _ in range(6)]
        prima = uview(12288, 6 * T * 2, BF16).rearrange("p (c t) -> p c t", t=T)
        prim = [V(prima[:, c], [prim_tks[c]]) for c in range(6)]
        stat = [V(uview(18432 + i * 2048, 2048, F32), [Tk()]) for i in range(4)]
        accb_ring = [V(uview(26624 + i * 1024, 1024, BF16), [Tk()]) for i in range(2)]
        lnt_ring = [V(uview(28672 + i * 2048, 2048, F32), [Tk()]) for i in range(2)]
        dg_ring = [V(uview(32768 + i * 256, 256, BF16), [Tk()]) for i in range(8)]
        mixA_views = acc + prim + stat + accb_ring + lnt_ring + dg_ring
        vtok_tks = [Tk() for _ in range(4)]
        vtoka = uview(0, 4 * 768 * 2, BF16).rearrange("p (a b) -> p a b", b=768)
        vtok = [V(vtoka[:, i], [vtok_tks[i]]) for i in range(4)]
        kb_ring = [V(uview(6144 + i * 1024, 1024, BF16), [Tk()]) for i in range(2)]
        kv_views = vtok + kb_ring
        qa_tks = [Tk() for _ in range(NH_B)]
        qaug_a = uview(0, NH_B * T * 2, BF16, npart=80).rearrange("p (h t) -> p h t", t=T)
        qaug = [V(qaug_a[:, h], [qa_tks[h]]) for h in range(NH_B)]
        qaug_all = V(qaug_a, qa_tks)
        co_tks = [Tk() for _ in range(NH_B // 2)]
        cato_a = uview(12288, (NH_B // 2) * T * 2, BF16).rearrange("p (h t) -> p h t", t=T)
        cato = [V(cato_a[:, n], [co_tks[n]]) for n in range(NH_B // 2)]
        G = V(uview(24576, 4 * NH_B * 16 * 4, F32).rearrange("p (q h n) -> p q h n", h=NH_B, n=16), [Tk()])
        Mb = []
        Mb_heads = []
        for i in range(2):
            a_ = uview(27648 + i * 1920, 1920, BF16).rearrange("p (h n) -> p h n", n=80)
            tks_ = [Tk() for _ in range(NH_B)]
            Mb.append(V(a_, tks_))
            Mb_heads.append([V(a_[:, h, :], [tks_[h]]) for h in range(NH_B)])
        Kt = [V(uview(31488 + i * 8192, 8192, BF16, npart=80), [Tk()]) for i in range(2)]
        Vt = [V(uview(47872 + i * 4160, 4160, BF16).rearrange("p (k d) -> p k d", d=HD + 1), [Tk()]) for i in range(2)]
        mixB_views = qaug + cato + [G] + Mb + Kt + Vt
        memf = V(uview(0, 8192, F32).rearrange("p (m f) -> p m f", f=D), [Tk()])
        memn = V(uview(8192, 4096, BF16).rearrange("p (m f) -> p m f", f=D), [Tk()])
        memT_tks = [Tk() for _ in range(8)]
        memTa = uview(12288, 4096, BF16).rearrange("p (c m) -> p c m", m=256)
        memT = [V(memTa[:, c], [memT_tks[c]]) for c in range(8)]
        mss = V(uview(16384, 8, F32), [Tk()])
        junk = V(uview(16640, 2048, BF16), [Tk()])
        init_views = [memf, memn, mss, junk] + memT

        sp_toks = []

        def ld(dst, src_ap, q="sp", chan="c", slow=False):
            return self.dma(q, chan, dst, V(src_ap, []), slow=slow)

        ld(ident_f, c_ident[:, :])
        ld(ropec, c_rope[:, :])
        gst = self.sb("gst", [56, 128], F32)
        for i, nm in enumerate(["ffn1_norm_g", "mix_norm_g", "ffn2_norm_g"]):
            ld(gst[16 * i:16 * i + 16, :], W[nm].rearrange("l (c p) -> (l c) p", p=128))
        ld(gst[48:56, :], W["kv_norm_g"].rearrange("(c p) -> c p", p=128))
        ld(ones_b, c_ones[:, :], q="pool", chan="cb")
        for e_ in ("pe", "act", "dve", "pool"):
            self.P.wait_all(e_, [("c_c", self.P.cnt["c_c"]), ("c_cb", self.P.cnt["c_cb"])])
        pgt = self.ps("M")
        self.tr(pgt[:, 0:56], gst, ident_f[0:56, 0:56])
        self.ts(gall, pgt[:, 0:56].rearrange("p (i c) -> p i c", c=8), 32.0)
        def late_param_loads():
            ld(perm_f, c_perm[:, :], chan="c2")
            ld(ident_b, c_ident[:, :], q="pool", chan="cb2")
            ld(tri_b, c_tri[:, :], q="pool", chan="cb2")
            ld(neg_b, c_neg[:, :], q="pool", chan="cb2")
            ld(onesblk_b, c_onesblk[:, :], q="pool", chan="cb2")
            ld(shup_b, c_ident[64:128, :], q="pool", chan="cb2")
            for l in range(2):
                ld(gmem[:, l, :], W["mem_norm_g"][l].rearrange("(c p) -> p c", p=128), chan="c2", slow=True)
            hsrc = [W["mem_q_norm_g"][0], W["mem_q_norm_g"][1], W["mem_k_norm_g"][0], W["mem_k_norm_g"][1],
                    W["k_norm_g"], W["b_q_norm_g"][0]]
            for i, g in enumerate(hsrc):
                ld(hg[0:64, i:i + 1], g.rearrange("(p o) -> p o", o=1), chan="c2")
                ld(hg[64:128, i:i + 1], g.rearrange("(p o) -> p o", o=1), chan="c2")
            for c in range(6):
                ld(dwk[:, c, :], W["a_dw_kernel"][0, :, 0, c * 128:(c + 1) * 128].rearrange("j p -> p j"), chan="c2", slow=True)
            for i, nm in enumerate(["a_dw_bias", "a_ln_g", "a_ln_b"]):
                ld(cvp[:, i, :], W[nm][0].rearrange("(c p) -> p c", p=128), chan="c2", slow=True)
        for c in range(6):
            self.memset(hglu[c][:, 0:HALO], 0.0)
        self.memset(kmeanT, 0.0)
        self.memset(ones_f, 1.0)

        def late_params():
            for e_ in ("pe", "act", "dve", "pool"):
                self.P.wait_all(e_, [("c_c2", self.P.cnt["c_c2"]), ("c_cb2", self.P.cnt["c_cb2"])])
            self.ts(hg[:, 2:5], hg[:, 2:5], 8.0)

        def norm(gi):
            psn = self.ps("M")
            for c in range(8):
                sq = self.nxt("sq")
                if c % 2 == 0:
                    self.act(sq, hT[c], AF.Square)
                else:
                    self.tt(sq, hT[c], hT[c], ALU.mult)
                self.mm(psn, ones_b, sq, start=(c == 0), stop=(c == 7))
            self.rsqrt_act(rbc, psn, float(D * EPS))
            for c in range(8):
                self.stt(xn[c], hT[c], gall[:, gi, c:c + 1], rbc, ALU.mult, ALU.mult)

        def proj(ps_out, w, lo, hi):
            for kc in range(8):
                self.mm(ps_out, w[:, kc, lo:hi], xn[kc], start=(kc == 0), stop=(kc == 7))

        def wchunk(wap, n):
            return self.wload(wap[:, n * 128:(n + 1) * 128].rearrange("(kc p) c -> p kc c", p=128), 128, (8, 128))

        def ffn(l, which, gi):
            uswitch(h1)
            norm(gi)
            wg, wu, wd = W[f"{which}_w_gate"][l], W[f"{which}_w_up"][l], W[f"{which}_w_down"][l]
            for f in range(NF):
                a = wchunk(wg, f)
                b = wchunk(wu, f)
                pg = self.ps("A")
                pu = self.ps("B")
                proj(pg, a, 0, 128)
                proj(pu, b, 0, 128)
                sg = self.nxt("sgt")
                self.act(sg, pg, AF.Silu)
                self.tt(h1[f], pu, sg, ALU.mult)
            for c in range(8):
                po = self.ps("S")
                f0 = 0
                for nf in (8, 8, 6):
                    w = self.wload(wd[f0 * 128:(f0 + nf) * 128, c * 128:(c + 1) * 128].rearrange("(f p) c -> p f c", p=128),
                                   128, (nf, 128))
                    for i in range(nf):
                        f = f0 + i
                        self.mm(po, w[:, i, :], h1[f], start=(f == 0), stop=(f == NF - 1))
                    f0 += nf
                self.stt(hT[c], po, 0.5, hT[c], ALU.mult, ALU.add)

        def pairs_batch(items):
            n = len(items)
            banks = [self.ps("A"), self.ps("A"), self.ps("B"), self.ps("B")][:n]
            qf = [self.nxt("qf") for _ in range(n)]
            sqh = [self.nxt("sqh") for _ in range(n)]
            rr = [self.nxt("rr") for _ in range(n)]
            t2 = [self.nxt("t2") for _ in range(n)]
            for i, it in enumerate(items):
                proj(banks[i], it[0], 0, 128)
            for i in range(n):
                self.copy(qf[i], banks[i], eng=("act" if i % 2 else "dve"))
            for i in range(n):
                self.tt(sqh[i], qf[i], qf[i], ALU.mult)
            for i in range(n):
                self.mm(banks[i], onesblk_b, sqh[i])
            for i in range(n):
                self.rsqrt_act(rr[i], banks[i], float(HD * EPS))
            for i, it in enumerate(items):
                self.stt(qf[i], qf[i], hg[:, it[1]:it[1] + 1], rr[i], ALU.mult, ALU.mult)
            rp = [i for i, it in enumerate(items) if it[2]]
            for i in rp:
                self.mm(banks[i], perm_f, qf[i])
            for i in rp:
                self.tt(t2[i], banks[i], ropeS, ALU.mult)
            for i in rp:
                self.tt(qf[i], qf[i], ropeC, ALU.mult)
            for i in rp:
                self.tt(qf[i], qf[i], t2[i], ALU.add)
            for i, it in enumerate(items):
                it[3](qf[i])

        def split_pair(q, dst_even, dst_odd):
            self.copy(dst_even, q[0:64, :], eng="act")
            qpb = self.nxt("qpb")
            self.copy(qpb[64:128, :], q[64:128, :])
            psh = self.ps("S")
            self.mm(psh[0:64, :], ident_b[64:128, 64:128], qpb[64:128, :])
            self.copy(dst_odd, psh[0:64, :], eng="act")

        def rope_tables(ti):
            ld(posi, pos[0:1, ti * T:(ti + 1) * T].partition_broadcast(128), chan="pos")
            self.copy(rta, posi)
            self.ts(rta, rta, ropec[:, 0:1])
            self.ts(rtb, rta, MAG, None, ALU.add)
            self.ts(rtb, rtb, -MAG, None, ALU.add)
            self.tt(rta, rta, rtb, ALU.subtract)
            self.act(ropeS, rta, AF.Sin, scale=6.28318)
            self.ts(ropeS, ropeS, ropec[:, 1:2])
            self.ts(rta, rta, 0.25, None, ALU.add)
            self.ts(rtb, rta, 0.5, None, ALU.is_gt)
            self.tt(rta, rta, rtb, ALU.subtract)
            self.act(ropeC, rta, AF.Sin, scale=6.28318)

        def put_head(dst_pair, h, po64, rec):
            if h % 2 == 0:
                self.tt(dst_pair[0:64, :], po64, rec, ALU.mult)
            else:
                ct = self.nxt("ctmp")
                self.tt(ct, po64, rec, ALU.mult)
                psh = self.ps("M")
                self.mm(psh, shup_b, ct)
                self.copy(dst_pair[64:128, :], psh[64:128, :], eng="act")

        def mem_attn(l):
            for h in range(NH_M):
                po = self.ps("A")
                pd = self.ps("B")
                for mt in range(2):
                    pss = self.ps("S")
                    self.mm(pss, memK[l][h][:, mt * 128:(mt + 1) * 128], qm[h])
                    pT = self.nxt("pT")
                    self.act(pT, pss, AF.Exp)
                    self.mm(po[0:64, :], memV[l][:, mt, h * 64:(h + 1) * 64], pT, start=(mt == 0), stop=(mt == 1))
                    self.mm(pd[0:64, :], ones_b[:, 0:64], pT, start=(mt == 0), stop=(mt == 1))
                rec = self.nxt("rec")
                self.recip_act(rec, pd[0:64, :])
                put_head(catm[h // 2], h, po[0:64, :], rec)

        def mem_q_items(wap, n0, gcol):
            return [(wchunk(wap, n0 + i), gcol, False,
                     (lambda q, i=i: split_pair(q, qm[2 * i], qm[2 * i + 1]))) for i in range(2)]

        def wo_proj(l, groups):
            wo = W["w_o"][l]
            for c in range(8):
                po = self.ps("S")
                n_mm = sum(len(g[3]) for g in groups)
                k = 0
                for (r0, npart, ng, rhs_list) in groups:
                    w = self.wload(wo[r0:r0 + npart * ng, c * 128:(c + 1) * 128].rearrange("(g p) c -> p g c", p=npart),
                                   npart, (ng, 128))
                    for gi_, r in enumerate(rhs_list):
                        self.mm(po, w[:, gi_, :], r, start=(k == 0), stop=(k == n_mm - 1))
                        k += 1
                self.tt(hT[c], po, hT[c], ALU.add)

        def mixer_a():
            uswitch(mixA_views)
            norm(2)
            win = W["a_w_in"][0]
            for c in range(6):
                wa = wchunk(win, c)
                wg = wchunk(win, 6 + c)
                pa = self.ps("A")
                pg = self.ps("B")
                proj(pa, wa, 0, 128)
                proj(pg, wg, 0, 128)
                sg = self.nxt("sgf")
                self.act(sg, pg, AF.Sigmoid)
                self.tt(hglu[c][:, HALO:HALO + T], pa, sg, ALU.mult)
            pairs_batch(mem_q_items(win, 12, 0))
            k_ = 0
            for c in range(6):
                pc = self.ps("A")
                for j in range(CONVW):
                    dg = dg_ring[k_ % 8]
                    k_ += 1
                    self.ts(dg, ident_f, dwk[:, c, j:j + 1])
                    self.mm(pc, dg, hglu[c][:, j:j + T], start=(j == 0), stop=(j == CONVW - 1))
                self.act(acc[c], pc, AF.Identity, bias=cvp[:, 0, c:c + 1])
            for c in range(6):
                self.copy(hglu[c][:, 0:HALO], hglu[c][:, T:T + HALO], eng="act")
            mem_attn(0)
            p1 = self.ps("A")
            p2 = self.ps("B")
            for c in range(6):
                ab = accb_ring[c % 2]
                sq = self.nxt("sq")
                self.copy(ab, acc[c], eng="act")
                self.act(sq, acc[c], AF.Square)
                self.mm(p1, ones_b, ab, start=(c == 0), stop=(c == 5))
                self.mm(p2, ones_b, sq, start=(c == 0), stop=(c == 5))
            mean, msq, rs, mrs = stat
            self.act(mean, p1, AF.Copy, scale=1.0 / 768)
            self.tt(msq, mean, mean, ALU.mult)
            self.stt(msq, p2, 1.0 / 768, msq, ALU.mult, ALU.subtract)
            self.rsqrt_act(rs, msq, float(EPS))
            self.tt(mrs, mean, rs, ALU.mult)
            for c in range(6):
                lt = lnt_ring[c % 2]
                self.tt(lt, acc[c], rs, ALU.mult)
                self.tt(lt, lt, mrs, ALU.subtract)
                self.act(prim[c], lt, AF.Silu, bias=cvp[:, 2, c:c + 1], scale=cvp[:, 1, c:c + 1])
            wo_proj(0, [(0, 128, 6, prim), (768, 128, 2, catm)])

        def kv_stage(ti):
            uswitch(kv_views)
            norm(6)
            wkv = W["w_kv"]
            for n0 in range(0, NH_B // 2, 3):
                ws = [wchunk(wkv, n0 + i) for i in range(3)]

                def k_consume(kr, n):
                    kb = kb_ring[n % 2]
                    self.copy(kb, kr, eng="act")
                    dst = V(kscr_t[2 * n:2 * n + 2].rearrange("h d s -> (h d) s")[:, ti * T:(ti + 1) * T],
                            kscr[2 * n].tks + kscr[2 * n + 1].tks)
                    self.dma("sp", f"ks{n}", dst, kb)
                    self.P.op("dve", lambda e, kr=kr: e.tensor_reduce(
                        out=km2.ap, in_=kr.ap.rearrange("p (n j) -> p n j", j=256), axis=AX.X, op=ALU.add),
                        reads=kr.tks, writes=km2.tks)
                    self.ts(kmeanT[:, n, 2 * ti:2 * ti + 2], km2, 1.0 / 256)
                pairs_batch([(ws[i], 4, True, (lambda q, n=n0 + i: k_consume(q, n))) for i in range(3)])
            wv = [self.wload(wkv[kc * 128:(kc + 1) * 128, 768:1536].rearrange("p (o c) -> p o c", o=1), 128, (1, 768))
                  for kc in range(8)]
            for tt_ in range(4):
                p1 = self.ps("A")
                p2 = self.ps("B")
                for kc in range(8):
                    lhs = xn[kc][:, tt_ * 128:(tt_ + 1) * 128]
                    self.mm(p1, lhs, wv[kc][:, 0, 0:512], start=(kc == 0), stop=(kc == 7))
                    self.mm(p2[:, 0:256], lhs, wv[kc][:, 0, 512:768], start=(kc == 0), stop=(kc == 7))
                self.copy(vtok[tt_][:, 0:512], p1, eng="act")
                self.copy(vtok[tt_][:, 512:768], p2[:, 0:256])
                r0 = ti * T + tt_ * 128
                self.dma("sp", f"vs{tt_}", V(vscr_t[r0:r0 + 128, :], [vscr_tk[ti]]), vtok[tt_])

        def mixer_b(ti):
            uswitch(mixB_views)
            norm(3)
            win = W["b_w_in"][0]
            gating = ti >= 2
            for i in range(2):
                ld(Kt[i][64:80, :], c_ind[:, :], q="pool", chan=f"ind{i}")
                self.memset(Vt[i][:, :, HD:HD + 1], 1.0)
            if not gating:
                self.memset(qaug_all[64:80], 0.0)
            mem_q_items_b = [None, None]
            for n0 in range(0, NH_B // 2, 3):
                ws = [wchunk(win, n0 + i) for i in range(3)]
                j_ = n0 // 3
                mem_q_items_b[j_] = (wchunk(win, 6 + j_), 1, False,
                                     (lambda q, j_=j_: split_pair(q, qm[2 * j_], qm[2 * j_ + 1])))

                def q_consume(qr, n):
                    split_pair(qr, qaug[2 * n][0:64, :], qaug[2 * n + 1][0:64, :])
                    if gating:
                        for par in range(2):
                            h = 2 * n + par
                            pg = self.ps("S")
                            for qt in range(4):
                                self.mm(pg[:, qt * 16:(qt + 1) * 16],
                                        qr[par * 64:(par + 1) * 64, qt * 128:(qt + 1) * 128],
                                        kmeanT[par * 64:(par + 1) * 64, n, :])
                            self.copy(G[:, :, h, :], pg[:, 0:64].rearrange("p (q n) -> p q n", n=16))
                mq = mem_q_items_b[n0 // 3]
                pairs_batch([(ws[i], 5, True, (lambda q, n=n0 + i: q_consume(q, n))) for i in range(3)] + [mq])
            if gating:
                for qt in range(4):
                    own = 2 * ti + qt // 2
                    self.memset(G[:, qt, :, own:16], -BIG)
                    mb = Mb[qt % 2]
                    mbh = Mb_heads[qt % 2]
                    for h in range(NH_B):
                        self.P.op("dve", lambda e, qt=qt, h=h: e.max(out=top8[h].ap, in_=G.ap[:, qt, h, :]),
                                  reads=G.tks, writes=top8[h].tks)
                    thr = V(top8_all.ap[:, :, 2:3].to_broadcast([128, NH_B, 16]), top8_all.tks)
                    self.tt(mb[:, :, 64:80], G[:, qt], thr, ALU.is_lt)
                    self.ts(mb[:, :, 64:80], mb[:, :, 64:80], -BIG)
                    self.memset(mb[:, :, 64 + own:65 + own], 0.0)
                    for h0 in range(0, NH_B, 4):
                        pt = self.ps("T")
                        for hh in range(4):
                            self.tr(pt[0:80, hh * 128:(hh + 1) * 128], mbh[h0 + hh], ident_b)
                        self.copy(qaug_all[64:80, h0:h0 + 4, qt * 128:(qt + 1) * 128],
                                  pt[64:80, 0:512].rearrange("p (h q) -> p h q", q=128))
            nkeys = (ti + 1) * T
            pending = []
            s3 = [self.psfam["S"][0][0], self.psfam["S"][0][1], self.psfam["B"][0][1]]
            s3i = [0]

            def nxt_s():
                v = s3[s3i[0] % 3]
                s3i[0] += 1
                return v

            def finalize(po, h):
                r1 = self.nxt("r1")
                self.recip_act(r1[64:65, :], po[64:65, :])
                pb = self.psfam["B"][0][0]
                self.mm(pb[0:64, :], ones_f[64:65, 0:64], r1[64:65, :])
                rec = self.nxt("rec")
                self.copy(rec, pb[0:64, :], eng="act")
                put_head(cato[h // 2], h, po[0:64, :], rec)

            npast = 4 * ti
            for h in range(NH_B):
                kt = Kt[h % 2]
                vt = Vt[h % 2]
                self.dma("sp", f"kt{h % 2}", kt[0:64, 0:nkeys], kscr[h][:, 0:nkeys])
                self.dma("sp", f"vt{h % 2}", vt[:, 0:nkeys // 128, 0:HD],
                         V(vscr_t[0:nkeys, h * HD:(h + 1) * HD].rearrange("(k p) d -> p k d", p=128),
                           vscr_tk[0:ti + 1]))
                po = self.ps("A")[0:HD + 1, :]
                items = []
                for j in range(npast):
                    def e_qk(pss, j=j):
                        self.mm(pss, kt[0:80, j * 128:(j + 1) * 128], qaug[h])
                    items.append((e_qk, [(j, slice(0, 512), slice(0, 512))]))
                for qb in range(2):
                    i = 2 * ti + qb
                    q0 = qb * 256
                    for n in range(2 * ti, i + 1):
                        def e_qk(pss, n=n, i=i, q0=q0):
                            for kk in range(2):
                                j = 2 * n + kk
                                ksl = kt[0:80, j * 128:(j + 1) * 128]
                                o = pss[:, kk * 256:(kk + 1) * 256]
                                if n < i:
                                    self.mm(o, ksl, qaug[h][:, q0:q0 + 256])
                                elif kk == 0:
                                    self.mm(o[:, 0:128], ksl, qaug[h][:, q0:q0 + 128], start=True, stop=False)
                                    self.mm(o[:, 0:128], ident_b, tri_b, start=False, stop=True)
                                    self.mm(o[:, 128:256], ksl, qaug[h][:, q0 + 128:q0 + 256])
                                else:
                                    self.mm(o[:, 0:128], ident_b, neg_b)
                                    self.mm(o[:, 128:256], ksl, qaug[h][:, q0 + 128:q0 + 256], start=True, stop=False)
                                    self.mm(o[:, 128:256], ident_b, tri_b, start=False, stop=True)
                        items.append((e_qk, [(2 * n + kk, slice(kk * 256, (kk + 1) * 256), slice(q0, q0 + 256))
                                             for kk in range(2)]))
                nit = len(items)
                npv = sum(len(it[1]) for it in items)
                pss_q = []
                for a in range(min(2, nit)):
                    p_ = nxt_s()
                    items[a][0](p_)
                    pss_q.append(p_)
                kpv = 0
                for a in range(nit):
                    if a + 2 < nit:
                        p_ = nxt_s()
                        items[a + 2][0](p_)
                        pss_q.append(p_)
                    cur = pss_q[a]
                    pT = self.nxt("pT")
                    self.act(pT, cur, AF.Exp)
                    for (j, psl, osl) in items[a][1]:
                        self.mm(po[:, osl], vt[:, j, :], pT[:, psl], start=(kpv == 0), stop=(kpv == npv - 1))
                        kpv += 1
                    if a == 0 and pending:
                        finalize(*pending.pop())
                pending.append((po, h))
            while pending:
                finalize(*pending.pop())
            mem_attn(1)
            wo_proj(1, [(0, 128, 8, cato + catm)])

        def load_tile(ti):
            uswitch(xs)
            for tt_ in range(4):
                s = xs[tt_ % 2]
                r0 = ti * T + tt_ * 128
                ld(s, x[r0:r0 + 128, :], chan=f"xs{tt_ % 2}")
                for half in range(2):
                    pt = self.ps("A")
                    for cc in range(4):
                        c = half * 4 + cc
                        self.tr(pt[:, cc * 128:(cc + 1) * 128], s[:, c * 128:(c + 1) * 128], ident_f)
                    for cc in range(4):
                        c = half * 4 + cc
                        self.copy(hT[c][:, tt_ * 128:(tt_ + 1) * 128], pt[:, cc * 128:(cc + 1) * 128],
                                  eng=("act" if cc % 2 else "dve"))

        def store_tile(ti):
            uswitch(xs)
            for tt_ in range(4):
                s = xs[tt_ % 2]
                for half in range(2):
                    pt = self.ps("A")
                    for cc in range(4):
                        c = half * 4 + cc
                        self.tr(pt[:, cc * 128:(cc + 1) * 128], hT[c][:, tt_ * 128:(tt_ + 1) * 128], ident_f)
                    self.copy(s[:, half * 512:(half + 1) * 512], pt, eng=("act" if half else "dve"))
                r0 = ti * T + tt_ * 128
                sp_toks.append(self.dma("sp", f"os{tt_ % 2}", V(out[r0:r0 + 128, :], []), s))

        def mem_kv_init(l):
            uswitch(init_views)
            for mt in range(2):
                ld(memf[:, mt, :], mem[mt * 128:(mt + 1) * 128, :], chan="mem")
            for mt in range(2):
                self.act(junk, memf[:, mt, :], AF.Square, accum=mss[:, mt:mt + 1])
            self.act(mss, mss, AF.Sqrt, bias=float(EPS), scale=1.0 / D)
            self.recip(mss, mss)
            for mt in range(2):
                self.ts(memn[:, mt, :], memf[:, mt, :], mss[:, mt:mt + 1])
            for mt in range(2):
                for c0 in (0, 4):
                    pt = self.ps("T")
                    for cc in range(4):
                        c = c0 + cc
                        self.tr(pt[:, cc * 128:(cc + 1) * 128], memn[:, mt, c * 128:(c + 1) * 128], ident_b)
                    for cc in range(4):
                        c = c0 + cc
                        self.ts(memT[c][:, mt * 128:(mt + 1) * 128], pt[:, cc * 128:(cc + 1) * 128], gmem[:, l, c:c + 1])
            wm = W["w_mem_kv"][l]
            for h in range(NH_M):
                if h % 2 == 0:
                    w = wchunk(wm, h // 2)
                pk = self.ps("A")
                for kc in range(8):
                    self.mm(pk[0:64, 0:256], w[:, kc, (h % 2) * 64:(h % 2) * 64 + 64], memT[kc],
                            start=(kc == 0), stop=(kc == 7))
                qf = self.nxt("qf")[0:64, 0:256]
                sqh = self.nxt("sqh")[0:64, 0:256]
                rr = self.nxt("rr")[0:64, 0:256]
                self.copy(qf, pk[0:64, 0:256], eng="act")
                self.tt(sqh, qf, qf, ALU.mult)
                pss = self.ps("M")
                self.mm(pss[0:64, 0:256], ones_b[0:64, 0:64], sqh)
                self.rsqrt_act(rr, pss[0:64, 0:256], float(HD * EPS))
                self.stt(memK[l][h], qf, hg[0:64, 2 + l:3 + l], rr, ALU.mult, ALU.mult)
            wv = [self.wload(wm[kc0 * 128:(kc0 + 4) * 128, 256:512].rearrange("(kc p) c -> p kc c", p=128), 128, (4, 256))
                  for kc0 in (0, 4)]
            for mt in range(2):
                pvv = self.ps("B")
                for kc in range(8):
                    self.mm(pvv[:, 0:256], memT[kc][:, mt * 128:(mt + 1) * 128], wv[kc // 4][:, kc % 4, :],
                            start=(kc == 0), stop=(kc == 7))
                self.copy(memV[l][:, mt, :], pvv[:, 0:256], eng="act")

        stages = ["ffn1_0", "mix_0", "ffn2_0", "kv", "ffn1_1", "mix_1", "ffn2_1"]
        nst = len(stages) if self.stop_after is None else (0 if self.stop_after == 'none' else stages.index(self.stop_after) + 1)
        for ti in range(self.n_tiles):
            load_tile(ti)
            if 'rope' not in _DBG:
                rope_tables(ti)
            if ti == 0:
                late_param_loads()
            for sname in stages[:nst]:
                if sname == "ffn1_0":
                    ffn(0, "ffn1", 0)
                    if ti == 0:
                        late_params()
                        mem_kv_init(0)
                        mem_kv_init(1)
                elif sname == "mix_0":
                    mixer_a()
                elif sname == "ffn2_0":
                    ffn(0, "ffn2", 4)
                elif sname == "kv":
                    kv_stage(ti)
                elif sname == "ffn1_1":
                    ffn(1, "ffn1", 1)
                elif sname == "mix_1":
                    mixer_b(ti)
                elif sname == "ffn2_1":
                    ffn(1, "ffn2", 5)
            store_tile(ti)
        self.P.wait_all("sp", sp_toks)
        self.P.emit(st)
        st.close()
        return nc


def _consts():
    ident = np.eye(128, dtype=np.float32)
    ones = np.ones((128, 128), np.float32)
    k = np.arange(128)[:, None]
    q = np.arange(128)[None, :]
    tri = np.where(k <= q, 0.0, -BIG).astype(np.float32)
    neg = np.full((128, 128), -BIG, np.float32)
    perm = np.zeros((128, 128), np.float32)
    onesblk = np.zeros((128, 128), np.float32)
    rope = np.zeros((128, 2), np.float32)
    inv_freq = 1.0 / (500000.0 ** (np.arange(0, 16, 2, dtype=np.float32) / 16.0))
    for b0 in (0, 64):
        onesblk[b0:b0 + 64, b0:b0 + 64] = 1.0
        for d in range(8):
            perm[b0 + d + 8, b0 + d] = 1.0
            perm[b0 + d, b0 + d + 8] = 1.0
        for d in range(16):
            rope[b0 + d, 0] = np.float32(inv_freq[d % 8]) / np.float32(2.0 * math.pi)
            rope[b0 + d, 1] = -1.0 if d < 8 else 1.0
    ind = np.zeros((16, S), np.float32)
    for n in range(16):
        ind[n, n * 256:(n + 1) * 256] = 1.0
    return {"c_ident": ident, "c_ones": ones, "c_tri": tri, "c_neg": neg, "c_perm": perm, "c_onesblk": onesblk,
            "c_rope": rope, "c_ind": ind}


_WNAMES = ["ffn1_norm_g", "ffn1_w_gate", "ffn1_w_up", "ffn1_w_down", "mix_norm_g", "mem_norm_g", "w_mem_kv",
           "mem_q_norm_g", "mem_k_norm_g", "w_o", "ffn2_norm_g", "ffn2_w_gate", "ffn2_w_up", "ffn2_w_down",
           "a_w_in", "a_dw_kernel", "a_dw_bias", "a_ln_g", "a_ln_b", "kv_norm_g", "w_kv", "k_norm_g",
           "b_w_in", "b_q_norm_g"]


def kernel(stop_after=None, n_cores=8, n_tiles=NT, skip_init=False, **inputs):
    nc = Builder(stop_after=stop_after, n_tiles=n_tiles, skip_init=skip_init).build()
    consts = _consts()
    shared = {n: np.ascontiguousarray(np.asarray(inputs[n], dtype=np.float32)) for n in _WNAMES}
    shared.update(consts)
    x = np.asarray(inputs["x"], dtype=np.float32)
    mem = np.asarray(inputs["mem"], dtype=np.float32)
    pos = np.asarray(inputs["positions"], dtype=np.int32)
    in_maps = []
    for b in range(n_cores):
        m = dict(shared)
        m["x"] = np.ascontiguousarray(x[b])
        m["mem"] = np.ascontiguousarray(mem[b])
        m["pos"] = np.ascontiguousarray(pos[b:b + 1])
        in_maps.append(m)
    res = run_bass_kernel_spmd(nc, in_maps, core_ids=list(range(n_cores)))
    return np.stack([np.asarray(r["out"], dtype=np.float32) for r in res.results], axis=0)
```

```python
import math
from contextlib import ExitStack

import numpy as np
import concourse.bass as bass
import concourse.mybir as mybir
from concourse.bass_utils import run_bass_kernel_spmd

F32 = mybir.dt.float32
BF16 = mybir.dt.bfloat16
I32 = mybir.dt.int32
AF = mybir.ActivationFunctionType
ALU = mybir.AluOpType
AX = mybir.AxisListType

D = 1024
S = 4096
DFF = 2816
NF = DFF // 128
T = 512
NT = S // T
HD = 64
NH_B = 12
NH_M = 4
CONVW = 31
HALO = CONVW - 1
EPS = 1e-6
BIG = 30000.0
MAG = 12582912.0
NW = 10
ENGS = ("pe", "act", "dve", "pool", "sp")
import os as _os
_DBG = _os.environ.get('DBG_SKIP', '')


class Tk:
    __slots__ = ("w", "rs", "excl")

    def __init__(self, excl=False):
        self.w = None
        self.rs = []
        self.excl = excl


class V:
    __slots__ = ("ap", "tks")

    def __init__(self, ap, tks):
        self.ap = ap
        self.tks = tks

    def __getitem__(self, idx):
        return V(self.ap[idx], self.tks)

    def rearrange(self, pat, **kw):
        return V(self.ap.rearrange(pat, **kw), self.tks)


class Prog:
    def __init__(self, nc):
        self.nc = nc
        self.ops = {e: [] for e in ENGS}
        self.cnt = {}
        self.seen = {e: {} for e in ENGS}
        self.semnames = []

    def _key(self, key):
        if key not in self.cnt:
            self.cnt[key] = 0
            self.semnames.append(key)
        return key

    def _deps(self, eng, reads, writes):
        deps = {}

        def add(tok):
            if tok is None:
                return
            k, v = tok
            if deps.get(k, 0) < v:
                deps[k] = v
        for t in reads:
            add(t.w)
            if t.excl:
                for r in t.rs:
                    if r[0] != eng:
                        add(r)
        for t in writes:
            add(t.w)
            for r in t.rs:
                add(r)
        waits = []
        for k, v in deps.items():
            if k == "pe" and eng == "pe":
                continue
            if self.seen[eng].get(k, 0) >= v:
                continue
            self.seen[eng][k] = v
            waits.append((k, v))
        return waits

    def _mark(self, tok, reads, writes):
        for t in reads:
            t.rs.append(tok)
            if len(t.rs) > 64:
                best = {}
                for k, v in t.rs:
                    if best.get(k, 0) < v:
                        best[k] = v
                t.rs = list(best.items())
        for t in writes:
            t.w = tok
            t.rs = []

    def op(self, eng, fn, reads=(), writes=(), inc=True):
        waits = self._deps(eng, reads, writes)
        self._key(eng)
        if inc:
            self.cnt[eng] += 1
            tok = (eng, self.cnt[eng])
        else:
            tok = (eng, self.cnt[eng] + 1)
        self.ops[eng].append((waits, fn, (eng, 1) if inc else None))
        self._mark(tok, reads, writes)
        return tok

    def dma(self, q, chan, fn, reads=(), writes=()):
        waits = self._deps(q, reads, writes)
        key = self._key("c_" + chan)
        self.cnt[key] += 16
        tok = (key, self.cnt[key])
        self.ops[q].append((waits, fn, (key, 16)))
        self._mark(tok, reads, writes)
        return tok

    def wait_all(self, eng, toks):
        waits = []
        for k, v in toks:
            if self.seen[eng].get(k, 0) >= v:
                continue
            self.seen[eng][k] = v
            waits.append((k, v))
        self.ops[eng].append((waits, None, None))

    def emit(self, stack):
        nc = self.nc
        sems = {}
        for key in self.semnames:
            sems[key] = stack.enter_context(nc.semaphore("s_" + key))
        block = stack.enter_context(nc.Block())

        def runner(name):
            def run(eng):
                for waits, fn, inc in self.ops[name]:
                    for k, v in waits:
                        eng.wait_ge(sems[k], v)
                    if fn is not None:
                        ins = fn(eng)
                        if inc is not None:
                            ins.then_inc(sems[inc[0]], inc[1])
            return run

        block.tensor(runner("pe"))
        block.scalar(runner("act"))
        block.vector(runner("dve"))
        block.gpsimd(runner("pool"))
        block.sync(runner("sp"))


def alias_switch(old_tks, new_tks):
    pend = {}
    for t in old_tks:
        toks = list(t.rs)
        if t.w is not None:
            toks.append(t.w)
        for k, v in toks:
            if pend.get(k, 0) < v:
                pend[k] = v
    pl = list(pend.items())
    for t in new_tks:
        t.w = None
        t.rs = list(pl)


class Builder:
    def __init__(self, stop_after=None, n_tiles=NT, skip_init=False):
        self.stop_after = stop_after
        self.n_tiles = n_tiles
        self.skip_init = skip_init
        self.nc = bass.Bass("TRN2", target_bir_lowering=False)
        self.P = Prog(self.nc)
        self.st = ExitStack()
        self.rings = {}

    def dram_in(self, name, shape, dt=F32):
        return self.nc.dram_tensor(name, list(shape), dt, kind="ExternalInput").ap()

    def sb(self, name, shape, dt):
        t = self.st.enter_context(self.nc.sbuf_tensor(name, list(shape), dt))
        return V(t[:], [Tk()])

    def sbc(self, name, shape, dt):
        t = self.st.enter_context(self.nc.sbuf_tensor(name, list(shape), dt))
        tks = [Tk() for _ in range(shape[1])]
        return [V(t[:, i], [tks[i]]) for i in range(shape[1])], V(t[:], tks)

    def ring(self, name, n, shape, dt):
        self.rings[name] = [[self.sb(f"{name}{i}", shape, dt) for i in range(n)], 0]

    def nxt(self, name):
        r = self.rings[name]
        v = r[0][r[1] % len(r[0])]
        r[1] += 1
        return v

    def mm(self, out, lhsT, rhs, start=True, stop=True):
        self.P.op("pe", lambda e: e.matmul(out.ap, lhsT=lhsT.ap, rhs=rhs.ap, start=start, stop=stop),
                  reads=lhsT.tks + rhs.tks, writes=out.tks, inc=True)

    def tr(self, out, in_, ident):
        self.P.op("pe", lambda e: e.transpose(out=out.ap, in_=in_.ap, identity=ident.ap),
                  reads=in_.tks + ident.tks, writes=out.tks)

    def act(self, out, in_, func, bias=None, scale=None, accum=None):
        reads = list(in_.tks)
        kw = {}
        if bias is not None:
            if isinstance(bias, V):
                reads += bias.tks
                kw["bias"] = bias.ap
            else:
                kw["bias"] = float(bias)
        if scale is not None:
            if isinstance(scale, V):
                reads += scale.tks
                kw["scale"] = scale.ap
            else:
                kw["scale"] = float(scale)
        writes = list(out.tks)
        if accum is not None:
            kw["accum_out"] = accum.ap
            writes += accum.tks
        self.P.op("act", lambda e: e.activation(out=out.ap, in_=in_.ap, func=func, **kw),
                  reads=reads, writes=writes)

    def tt(self, out, a, b, op, eng="dve"):
        self.P.op(eng, lambda e: e.tensor_tensor(out=out.ap, in0=a.ap, in1=b.ap, op=op),
                  reads=a.tks + b.tks, writes=out.tks)

    def ts(self, out, a, s1, s2=None, op0=ALU.mult, op1=None, eng="dve"):
        reads = list(a.tks)

        def low(s):
            if isinstance(s, V):
                reads.extend(s.tks)
                return s.ap
            return None if s is None else float(s)
        a1, a2 = low(s1), low(s2)
        if op1 is None:
            self.P.op(eng, lambda e: e.tensor_scalar(out=out.ap, in0=a.ap, scalar1=a1, scalar2=None, op0=op0),
                      reads=reads, writes=out.tks)
        else:
            self.P.op(eng, lambda e: e.tensor_scalar(out=out.ap, in0=a.ap, scalar1=a1, scalar2=a2, op0=op0, op1=op1),
                      reads=reads, writes=out.tks)

    def stt(self, out, a, scalar, b, op0, op1, eng="dve"):
        reads = a.tks + b.tks
        if isinstance(scalar, V):
            reads = reads + scalar.tks
            sc = scalar.ap
        else:
            sc = float(scalar)
        self.P.op(eng, lambda e: e.scalar_tensor_tensor(out=out.ap, in0=a.ap, scalar=sc, in1=b.ap, op0=op0, op1=op1),
                  reads=reads, writes=out.tks)

    def copy(self, out, in_, eng="dve"):
        if eng == "act":
            self.act(out, in_, AF.Copy)
        else:
            self.P.op(eng, lambda e: e.tensor_copy(out=out.ap, in_=in_.ap), reads=in_.tks, writes=out.tks)

    def memset(self, out, val, eng="dve"):
        self.P.op(eng, lambda e: e.memset(out.ap, val), writes=out.tks)

    def rsqrt_act(self, out, in_, bias):
        self.act(out, in_, AF.Ln, bias=bias, scale=1.0)
        self.act(out, out, AF.Exp, scale=-0.5)

    def recip_act(self, out, in_):
        self.act(out, in_, AF.Ln)
        self.act(out, out, AF.Exp, scale=-1.0)

    def recip(self, out, in_):
        self.P.op("dve", lambda e: e.reciprocal(out=out.ap, in_=in_.ap), reads=in_.tks, writes=out.tks)

    def dma(self, q, chan, out, in_, slow=False):
        if slow:
            return self.P.dma(q, chan, lambda e: e.dma_start(out=out.ap, in_=in_.ap, allow_slow_non_contiguous=True),
                              reads=in_.tks, writes=out.tks)
        return self.P.dma(q, chan, lambda e: e.dma_start(out=out.ap, in_=in_.ap), reads=in_.tks, writes=out.tks)

    def ps(self, fam):
        r = self.psfam[fam]
        v = r[0][r[1] % len(r[0])]
        r[1] += 1
        return v

    def wload(self, src_ap, npart, dims):
        i = self.wi % NW
        self.wi += 1
        slot = self.wslots[i]
        n = dims[0] * dims[1]
        view = V(slot.ap[0:npart, 0:n].rearrange("p (a b) -> p a b", b=dims[1]), slot.tks)
        self.dma("pool", f"w{i}", view, V(src_ap, []))
        return view

    def build(self):
        nc = self.nc
        st = self.st
        B_ = self
        x = self.dram_in("x", [S, D])
        mem = self.dram_in("mem", [256, D])
        pos = self.dram_in("pos", [1, S], I32)
        W = {}
        for name, shape in [
            ("ffn1_norm_g", [2, D]), ("ffn1_w_gate", [2, D, DFF]), ("ffn1_w_up", [2, D, DFF]), ("ffn1_w_down", [2, DFF, D]),
            ("mix_norm_g", [2, D]), ("mem_norm_g", [2, D]), ("w_mem_kv", [2, D, 512]),
            ("mem_q_norm_g", [2, HD]), ("mem_k_norm_g", [2, HD]), ("w_o", [2, D, D]),
            ("ffn2_norm_g", [2, D]), ("ffn2_w_gate", [2, D, DFF]), ("ffn2_w_up", [2, D, DFF]), ("ffn2_w_down", [2, DFF, D]),
            ("a_w_in", [1, D, 1792]), ("a_dw_kernel", [1, CONVW, 1, 768]), ("a_dw_bias", [1, 768]),
            ("a_ln_g", [1, 768]), ("a_ln_b", [1, 768]), ("kv_norm_g", [D]), ("w_kv", [D, 1536]),
            ("k_norm_g", [HD]), ("b_w_in", [1, D, D]), ("b_q_norm_g", [1, HD]),
        ]:
            W[name] = self.dram_in(name, shape)
        c_ident = self.dram_in("c_ident", [128, 128])
        c_ones = self.dram_in("c_ones", [128, 128])
        c_tri = self.dram_in("c_tri", [128, 128])
        c_neg = self.dram_in("c_neg", [128, 128])
        c_perm = self.dram_in("c_perm", [128, 128])
        c_onesblk = self.dram_in("c_onesblk", [128, 128])
        c_rope = self.dram_in("c_rope", [128, 2])
        c_ind = self.dram_in("c_ind", [16, S])
        out = nc.dram_tensor("out", [S, D], F32, kind="ExternalOutput").ap()
        kscr_t = nc.dram_tensor("kscr", [NH_B, HD, S], BF16, kind="Internal").ap()
        vscr_t = nc.dram_tensor("vscr", [S, NH_B * HD], BF16, kind="Internal").ap()
        kscr = [V(kscr_t[h], [Tk()]) for h in range(NH_B)]
        vscr_tk = [Tk() for _ in range(NT)]

        psb = [st.enter_context(nc.psum_tensor(f"psb{i}", [128, 512], F32)) for i in range(7)]
        pst = st.enter_context(nc.psum_tensor("pst", [128, 1024], BF16))
        pv = [V(p[:], [Tk(excl=True)]) for p in psb]
        self.psfam = {"A": [pv[0:2], 0], "B": [pv[2:4], 0], "S": [pv[4:6], 0], "M": [pv[6:7], 0],
                      "T": [[V(pst[:], [Tk(excl=True)])], 0]}

        hT, hT_all = self.sbc("hT", [128, 8, T], F32)
        xn, xn_all = self.sbc("xn", [128, 8, T], BF16)
        rbc = self.sb("rbc", [128, T], F32)
        self.ring("sq", 4, [128, T], BF16)
        self.ring("sgt", 2, [128, T], BF16)
        self.ring("sgf", 2, [128, T], F32)
        self.wslots = [self.sb(f"wslot{i}", [128, 1024], BF16) for i in range(NW)]
        self.wi = 0
        hglu, hglu_all = self.sbc("hglu", [128, 6, HALO + T], BF16)
        for nm, dt in [("qf", F32), ("rr", F32), ("t2", F32)]:
            self.ring(nm, 4, [128, T], dt)
        self.ring("sqh", 4, [128, T], BF16)
        self.ring("qpb", 2, [128, T], BF16)
        qm, qm_all = self.sbc("qm", [64, NH_M, T], BF16)
        catm, catm_all = self.sbc("catm", [128, NH_M // 2, T], BF16)
        self.ring("ctmp", 2, [64, T], BF16)
        shup_b = self.sb("shup_b", [64, 128], BF16)
        memK = [self.sbc(f"memK{l}", [64, NH_M, 256], BF16)[0] for l in range(2)]
        memV = [self.sb(f"memV{l}", [128, 2, 256], BF16) for l in range(2)]
        self.ring("pT", 4, [128, T], BF16)
        self.ring("rec", 2, [64, T], F32)
        ropeC = self.sb("ropeC", [128, T], F32)
        ropeS = self.sb("ropeS", [128, T], F32)
        posi = self.sb("posi", [128, T], I32)
        rta = self.sb("rta", [128, T], F32)
        rtb = self.sb("rtb", [128, T], F32)
        kmeanT = self.sb("kmeanT", [128, NH_B // 2, 16], F32)
        km2 = self.sb("km2", [128, 2], F32)
        top8, top8_all = self.sbc("top8", [128, NH_B, 8], F32)
        ident_f = self.sb("ident_f", [128, 128], F32)
        ident_b = self.sb("ident_b", [128, 128], BF16)
        ones_b = self.sb("ones_b", [128, 128], BF16)
        tri_b = self.sb("tri_b", [128, 128], BF16)
        neg_b = self.sb("neg_b", [128, 128], BF16)
        perm_f = self.sb("perm_f", [128, 128], F32)
        onesblk_b = self.sb("onesblk_b", [128, 128], BF16)
        ones_f = self.sb("ones_f", [128, 64], F32)
        self.ring("r1", 2, [128, T], F32)
        ropec = self.sb("ropec", [128, 2], F32)
        gall = self.sb("gall", [128, 7, 8], F32)
        gmem = self.sb("gmem", [128, 2, 8], F32)
        hg = self.sb("hg", [128, 8], F32)
        dwk = self.sb("dwk", [128, 6, CONVW], F32)
        cvp = self.sb("cvp", [128, 3, 6], F32)

        USZ = 28160
        Ut = st.enter_context(nc.sbuf_tensor("U", [128, USZ], BF16))
        ucur = {"tks": []}

        def uview(off_b, nbytes, dt, npart=128, p0=0):
            a = Ut[p0:p0 + npart, off_b // 2:(off_b + nbytes) // 2]
            if dt == F32:
                a = a.bitcast(F32)
            elif dt == I32:
                a = a.bitcast(I32)
            return a

        def uswitch(views):
            tks = []
            for v in views:
                for t in v.tks:
                    if t not in tks:
                        tks.append(t)
            alias_switch(ucur["tks"], tks)
            ucur["tks"] = tks

        h1_tks = [Tk() for _ in range(NF)]
        h1a = uview(0, NF * T * 2, BF16).rearrange("p (f t) -> p f t", t=T)
        h1 = [V(h1a[:, f], [h1_tks[f]]) for f in range(NF)]
        xs = [V(uview(i * 4096, 4096, F32), [Tk()]) for i in range(2)]
        acc_tks = [Tk() for _ in range(6)]
        acca = uview(0, 6 * T * 4, F32).rearrange("p (c t) -> p c t", t=T)
        acc = [V(acca[:, c], [acc_tks[c]]) for c in range(6)]
        prim_tks = [Tk() for _ in range(6)]
        prima = uview(12288, 6 * T * 2, BF16).rearrange("p (c t) -> p c t", t=T)
        prim = [V(prima[:, c], [prim_tks[c]]) for c in range(6)]
        stat = [V(uview(18432 + i * 2048, 2048, F32), [Tk()]) for i in range(4)]
        accb_ring = [V(uview(26624 + i * 1024, 1024, BF16), [Tk()]) for i in range(2)]
        lnt_ring = [V(uview(28672 + i * 2048, 2048, F32), [Tk()]) for i in range(2)]
        dg_ring = [V(uview(32768 + i * 256, 256, BF16), [Tk()]) for i in range(8)]
        mixA_views = acc + prim + stat + accb_ring + lnt_ring + dg_ring
        vtok_tks = [Tk() for _ in range(4)]
        vtoka = uview(0, 4 * 768 * 2, BF16).rearrange("p (a b) -> p a b", b=768)
        vtok = [V(vtoka[:, i], [vtok_tks[i]]) for i in range(4)]
        kb_ring = [V(uview(6144 + i * 1024, 1024, BF16), [Tk()]) for i in range(2)]
        kv_views = vtok + kb_ring
        qa_tks = [Tk() for _ in range(NH_B)]
        qaug_a = uview(0, NH_B * T * 2, BF16, npart=80).rearrange("p (h t) -> p h t", t=T)
        qaug = [V(qaug_a[:, h], [qa_tks[h]]) for h in range(NH_B)]
        qaug_all = V(qaug_a, qa_tks)
        co_tks = [Tk() for _ in range(NH_B // 2)]
        cato_a = uview(12288, (NH_B // 2) * T * 2, BF16).rearrange("p (h t) -> p h t", t=T)
        cato = [V(cato_a[:, n], [co_tks[n]]) for n in range(NH_B // 2)]
        G = V(uview(24576, 4 * NH_B * 16 * 4, F32).rearrange("p (q h n) -> p q h n", h=NH_B, n=16), [Tk()])
        Mb = []
        Mb_heads = []
        for i in range(2):
            a_ = uview(27648 + i * 1920, 1920, BF16).rearrange("p (h n) -> p h n", n=80)
            tks_ = [Tk() for _ in range(NH_B)]
            Mb.append(V(a_, tks_))
            Mb_heads.append([V(a_[:, h, :], [tks_[h]]) for h in range(NH_B)])
        Kt = [V(uview(31488 + i * 8192, 8192, BF16, npart=80), [Tk()]) for i in range(2)]
        Vt = [V(uview(47872 + i * 4160, 4160, BF16).rearrange("p (k d) -> p k d", d=HD + 1), [Tk()]) for i in range(2)]
        mixB_views = qaug + cato + [G] + Mb + Kt + Vt
        memf = V(uview(0, 8192, F32).rearrange("p (m f) -> p m f", f=D), [Tk()])
        memn = V(uview(8192, 4096, BF16).rearrange("p (m f) -> p m f", f=D), [Tk()])
        memT_tks = [Tk() for _ in range(8)]
        memTa = uview(12288, 4096, BF16).rearrange("p (c m) -> p c m", m=256)
        memT = [V(memTa[:, c], [memT_tks[c]]) for c in range(8)]
        mss = V(uview(16384, 8, F32), [Tk()])
        junk = V(uview(16640, 2048, BF16), [Tk()])
        init_views = [memf, memn, mss, junk] + memT

        sp_toks = []
        xpre = [self.sb(f"xpre{i}", [128, D], F32) for i in range(4)]

        def ld(dst, src_ap, q="sp", chan="c", slow=False):
            return self.dma(q, chan, dst, V(src_ap, []), slow=slow)

        ld(ident_f, c_ident[:, :])
        ld(ropec, c_rope[:, :])
        gst = self.sb("gst", [56, 128], F32)
        for i, nm in enumerate(["ffn1_norm_g", "mix_norm_g", "ffn2_norm_g"]):
            ld(gst[16 * i:16 * i + 16, :], W[nm].rearrange("l (c p) -> (l c) p", p=128))
        ld(gst[48:56, :], W["kv_norm_g"].rearrange("(c p) -> c p", p=128))
        ld(ones_b, c_ones[:, :], q="pool", chan="cb")
        for e_ in ("pe", "act", "dve", "pool"):
            self.P.wait_all(e_, [("c_c", self.P.cnt["c_c"]), ("c_cb", self.P.cnt["c_cb"])])
        pgt = self.ps("M")
        self.tr(pgt[:, 0:56], gst, ident_f[0:56, 0:56])
        self.ts(gall, pgt[:, 0:56].rearrange("p (i c) -> p i c", c=8), 32.0)
        def late_param_loads():
            ld(perm_f, c_perm[:, :], chan="c2")
            ld(ident_b, c_ident[:, :], q="pool", chan="cb2")
            ld(tri_b, c_tri[:, :], q="pool", chan="cb2")
            ld(neg_b, c_neg[:, :], q="pool", chan="cb2")
            ld(onesblk_b, c_onesblk[:, :], q="pool", chan="cb2")
            ld(shup_b, c_ident[64:128, :], q="pool", chan="cb2")
            for l in range(2):
                ld(gmem[:, l, :], W["mem_norm_g"][l].rearrange("(c p) -> p c", p=128), chan="c2", slow=True)
            hsrc = [W["mem_q_norm_g"][0], W["mem_q_norm_g"][1], W["mem_k_norm_g"][0], W["mem_k_norm_g"][1],
                    W["k_norm_g"], W["b_q_norm_g"][0]]
            for i, g in enumerate(hsrc):
                ld(hg[0:64, i:i + 1], g.rearrange("(p o) -> p o", o=1), chan="c2")
                ld(hg[64:128, i:i + 1], g.rearrange("(p o) -> p o", o=1), chan="c2")
            for c in range(6):
                ld(dwk[:, c, :], W["a_dw_kernel"][0, :, 0, c * 128:(c + 1) * 128].rearrange("j p -> p j"), chan="c2", slow=True)
            for i, nm in enumerate(["a_dw_bias", "a_ln_g", "a_ln_b"]):
                ld(cvp[:, i, :], W[nm][0].rearrange("(c p) -> p c", p=128), chan="c2", slow=True)
        for c in range(6):
            self.memset(hglu[c][:, 0:HALO], 0.0)
        self.memset(kmeanT, 0.0)
        self.memset(ones_f, 1.0)

        def late_params():
            for e_ in ("pe", "act", "dve", "pool"):
                self.P.wait_all(e_, [("c_c2", self.P.cnt["c_c2"]), ("c_cb2", self.P.cnt["c_cb2"])])
            self.ts(hg[:, 2:5], hg[:, 2:5], 8.0)

        def norm(gi):
            psn = self.ps("M")
            for c in range(8):
                sq = self.nxt("sq")
                if c % 2 == 0:
                    self.act(sq, hT[c], AF.Square)
                else:
                    self.tt(sq, hT[c], hT[c], ALU.mult)
                self.mm(psn, ones_b, sq, start=(c == 0), stop=(c == 7))
            self.rsqrt_act(rbc, psn, float(D * EPS))
            for c in range(8):
                self.stt(xn[c], hT[c], gall[:, gi, c:c + 1], rbc, ALU.mult, ALU.mult)

        def proj(ps_out, w, lo, hi):
            for kc in range(8):
                self.mm(ps_out, w[:, kc, lo:hi], xn[kc], start=(kc == 0), stop=(kc == 7))

        def wchunk(wap, n):
            return self.wload(wap[:, n * 128:(n + 1) * 128].rearrange("(kc p) c -> p kc c", p=128), 128, (8, 128))

        def ffn(l, which, gi):
            uswitch(h1)
            norm(gi)
            wg, wu, wd = W[f"{which}_w_gate"][l], W[f"{which}_w_up"][l], W[f"{which}_w_down"][l]
            for f in range(NF):
                a = wchunk(wg, f)
                b = wchunk(wu, f)
                pg = self.ps("A")
                pu = self.ps("B")
                proj(pg, a, 0, 128)
                proj(pu, b, 0, 128)
                sg = self.nxt("sgt")
                self.act(sg, pg, AF.Silu)
                self.tt(h1[f], pu, sg, ALU.mult)
            for c in range(8):
                po = self.ps("S")
                f0 = 0
                for nf in (8, 8, 6):
                    w = self.wload(wd[f0 * 128:(f0 + nf) * 128, c * 128:(c + 1) * 128].rearrange("(f p) c -> p f c", p=128),
                                   128, (nf, 128))
                    for i in range(nf):
                        f = f0 + i
                        self.mm(po, w[:, i, :], h1[f], start=(f == 0), stop=(f == NF - 1))
                    f0 += nf
                self.stt(hT[c], po, 0.5, hT[c], ALU.mult, ALU.add)

        def pairs_batch(items):
            n = len(items)
            banks = [self.ps("A"), self.ps("A"), self.ps("B"), self.ps("B")][:n]
            qf = [self.nxt("qf") for _ in range(n)]
            sqh = [self.nxt("sqh") for _ in range(n)]
            rr = [self.nxt("rr") for _ in range(n)]
            t2 = [self.nxt("t2") for _ in range(n)]
            for i, it in enumerate(items):
                proj(banks[i], it[0], 0, 128)
            for i in range(n):
                self.copy(qf[i], banks[i], eng=("act" if i % 2 else "dve"))
            for i in range(n):
                self.tt(sqh[i], qf[i], qf[i], ALU.mult)
            for i in range(n):
                self.mm(banks[i], onesblk_b, sqh[i])
            for i in range(n):
                self.rsqrt_act(rr[i], banks[i], float(HD * EPS))
            for i, it in enumerate(items):
                self.stt(qf[i], qf[i], hg[:, it[1]:it[1] + 1], rr[i], ALU.mult, ALU.mult)
            rp = [i for i, it in enumerate(items) if it[2]]
            for i in rp:
                self.mm(banks[i], perm_f, qf[i])
            for i in rp:
                self.tt(t2[i], banks[i], ropeS, ALU.mult)
            for i in rp:
                self.tt(qf[i], qf[i], ropeC, ALU.mult)
            for i in rp:
                self.tt(qf[i], qf[i], t2[i], ALU.add)
            for i, it in enumerate(items):
                it[3](qf[i])

        def split_pair(q, dst_even, dst_odd):
            self.copy(dst_even, q[0:64, :], eng="act")
            qpb = self.nxt("qpb")
            self.copy(qpb[64:128, :], q[64:128, :])
            psh = self.ps("S")
            self.mm(psh[0:64, :], ident_b[64:128, 64:128], qpb[64:128, :])
            self.copy(dst_odd, psh[0:64, :], eng="act")

        def rope_tables(ti):
            ld(posi, pos[0:1, ti * T:(ti + 1) * T].partition_broadcast(128), chan="pos")
            self.copy(rta, posi)
            self.ts(rta, rta, ropec[:, 0:1])
            self.ts(rtb, rta, MAG, None, ALU.add)
            self.ts(rtb, rtb, -MAG, None, ALU.add)
            self.tt(rta, rta, rtb, ALU.subtract)
            self.act(ropeS, rta, AF.Sin, scale=6.28318)
            self.ts(ropeS, ropeS, ropec[:, 1:2])
            self.ts(rta, rta, 0.25, None, ALU.add)
            self.ts(rtb, rta, 0.5, None, ALU.is_gt)
            self.tt(rta, rta, rtb, ALU.subtract)
            self.act(ropeC, rta, AF.Sin, scale=6.28318)

        def put_head(dst_pair, h, po64, rec):
            if h % 2 == 0:
                self.tt(dst_pair[0:64, :], po64, rec, ALU.mult)
            else:
                ct = self.nxt("ctmp")
                self.tt(ct, po64, rec, ALU.mult)
                psh = self.ps("M")
                self.mm(psh, shup_b, ct)
                self.copy(dst_pair[64:128, :], psh[64:128, :], eng="act")

        def mem_attn(l):
            for h in range(NH_M):
                po = self.ps("A")
                pd = self.ps("B")
                for mt in range(2):
                    pss = self.ps("S")
                    self.mm(pss, memK[l][h][:, mt * 128:(mt + 1) * 128], qm[h])
                    pT = self.nxt("pT")
                    self.act(pT, pss, AF.Exp)
                    self.mm(po[0:64, :], memV[l][:, mt, h * 64:(h + 1) * 64], pT, start=(mt == 0), stop=(mt == 1))
                    self.mm(pd[0:64, :], ones_b[:, 0:64], pT, start=(mt == 0), stop=(mt == 1))
                rec = self.nxt("rec")
                self.recip_act(rec, pd[0:64, :])
                put_head(catm[h // 2], h, po[0:64, :], rec)

        def mem_q_items(wap, n0, gcol):
            return [(wchunk(wap, n0 + i), gcol, False,
                     (lambda q, i=i: split_pair(q, qm[2 * i], qm[2 * i + 1]))) for i in range(2)]

        def wo_proj(l, groups):
            wo = W["w_o"][l]
            for c in range(8):
                po = self.ps("S")
                n_mm = sum(len(g[3]) for g in groups)
                k = 0
                for (r0, npart, ng, rhs_list) in groups:
                    w = self.wload(wo[r0:r0 + npart * ng, c * 128:(c + 1) * 128].rearrange("(g p) c -> p g c", p=npart),
                                   npart, (ng, 128))
                    for gi_, r in enumerate(rhs_list):
                        self.mm(po, w[:, gi_, :], r, start=(k == 0), stop=(k == n_mm - 1))
                        k += 1
                self.tt(hT[c], po, hT[c], ALU.add)

        def mixer_a():
            uswitch(mixA_views)
            norm(2)
            win = W["a_w_in"][0]
            for c in range(6):
                wa = wchunk(win, c)
                wg = wchunk(win, 6 + c)
                pa = self.ps("A")
                pg = self.ps("B")
                proj(pa, wa, 0, 128)
                proj(pg, wg, 0, 128)
                sg = self.nxt("sgf")
                self.act(sg, pg, AF.Sigmoid)
                self.tt(hglu[c][:, HALO:HALO + T], pa, sg, ALU.mult)
            pairs_batch(mem_q_items(win, 12, 0))
            k_ = 0
            for c in range(6):
                pc = self.ps("A")
                for j in range(CONVW):
                    dg = dg_ring[k_ % 8]
                    k_ += 1
                    self.ts(dg, ident_f, dwk[:, c, j:j + 1])
                    self.mm(pc, dg, hglu[c][:, j:j + T], start=(j == 0), stop=(j == CONVW - 1))
                self.act(acc[c], pc, AF.Identity, bias=cvp[:, 0, c:c + 1])
            for c in range(6):
                self.copy(hglu[c][:, 0:HALO], hglu[c][:, T:T + HALO], eng="act")
            mem_attn(0)
            p1 = self.ps("A")
            p2 = self.ps("B")
            for c in range(6):
                ab = accb_ring[c % 2]
                sq = self.nxt("sq")
                self.copy(ab, acc[c], eng="act")
                self.act(sq, acc[c], AF.Square)
                self.mm(p1, ones_b, ab, start=(c == 0), stop=(c == 5))
                self.mm(p2, ones_b, sq, start=(c == 0), stop=(c == 5))
            mean, msq, rs, mrs = stat
            self.act(mean, p1, AF.Copy, scale=1.0 / 768)
            self.tt(msq, mean, mean, ALU.mult)
            self.stt(msq, p2, 1.0 / 768, msq, ALU.mult, ALU.subtract)
            self.rsqrt_act(rs, msq, float(EPS))
            self.tt(mrs, mean, rs, ALU.mult)
            for c in range(6):
                lt = lnt_ring[c % 2]
                self.tt(lt, acc[c], rs, ALU.mult)
                self.tt(lt, lt, mrs, ALU.subtract)
                self.act(prim[c], lt, AF.Silu, bias=cvp[:, 2, c:c + 1], scale=cvp[:, 1, c:c + 1])
            wo_proj(0, [(0, 128, 6, prim), (768, 128, 2, catm)])

        def kv_stage(ti):
            uswitch(kv_views)
            norm(6)
            wkv = W["w_kv"]
            for n0 in range(0, NH_B // 2, 3):
                ws = [wchunk(wkv, n0 + i) for i in range(3)]

                def k_consume(kr, n):
                    kb = kb_ring[n % 2]
                    self.copy(kb, kr, eng="act")
                    dst = V(kscr_t[2 * n:2 * n + 2].rearrange("h d s -> (h d) s")[:, ti * T:(ti + 1) * T],
                            kscr[2 * n].tks + kscr[2 * n + 1].tks)
                    self.dma("sp", f"ks{n}", dst, kb)
                    self.P.op("dve", lambda e, kr=kr: e.tensor_reduce(
                        out=km2.ap, in_=kr.ap.rearrange("p (n j) -> p n j", j=256), axis=AX.X, op=ALU.add),
                        reads=kr.tks, writes=km2.tks)
                    self.ts(kmeanT[:, n, 2 * ti:2 * ti + 2], km2, 1.0 / 256)
                pairs_batch([(ws[i], 4, True, (lambda q, n=n0 + i: k_consume(q, n))) for i in range(3)])
            wv = [self.wload(wkv[kc * 128:(kc + 1) * 128, 768:1536].rearrange("p (o c) -> p o c", o=1), 128, (1, 768))
                  for kc in range(8)]
            for tt_ in range(4):
                p1 = self.ps("A")
                p2 = self.ps("B")
                for kc in range(8):
                    lhs = xn[kc][:, tt_ * 128:(tt_ + 1) * 128]
                    self.mm(p1, lhs, wv[kc][:, 0, 0:512], start=(kc == 0), stop=(kc == 7))
                    self.mm(p2[:, 0:256], lhs, wv[kc][:, 0, 512:768], start=(kc == 0), stop=(kc == 7))
                self.copy(vtok[tt_][:, 0:512], p1, eng="act")
                self.copy(vtok[tt_][:, 512:768], p2[:, 0:256])
                r0 = ti * T + tt_ * 128
                self.dma("sp", f"vs{tt_}", V(vscr_t[r0:r0 + 128, :], [vscr_tk[ti]]), vtok[tt_])

        def mixer_b(ti):
            uswitch(mixB_views)
            norm(3)
            win = W["b_w_in"][0]
            gating = ti >= 2
            for i in range(2):
                ld(Kt[i][64:80, :], c_ind[:, :], q="pool", chan=f"ind{i}")
                self.memset(Vt[i][:, :, HD:HD + 1], 1.0)
            if not gating:
                self.memset(qaug_all[64:80], 0.0)
            mem_q_items_b = [None, None]
            for n0 in range(0, NH_B // 2, 3):
                ws = [wchunk(win, n0 + i) for i in range(3)]
                j_ = n0 // 3
                mem_q_items_b[j_] = (wchunk(win, 6 + j_), 1, False,
                                     (lambda q, j_=j_: split_pair(q, qm[2 * j_], qm[2 * j_ + 1])))

                def q_consume(qr, n):
                    split_pair(qr, qaug[2 * n][0:64, :], qaug[2 * n + 1][0:64, :])
                    if gating:
                        for par in range(2):
                            h = 2 * n + par
                            pg = self.ps("S")
                            for qt in range(4):
                                self.mm(pg[:, qt * 16:(qt + 1) * 16],
                                        qr[par * 64:(par + 1) * 64, qt * 128:(qt + 1) * 128],
                                        kmeanT[par * 64:(par + 1) * 64, n, :])
                            self.copy(G[:, :, h, :], pg[:, 0:64].rearrange("p (q n) -> p q n", n=16))
                mq = mem_q_items_b[n0 // 3]
                pairs_batch([(ws[i], 5, True, (lambda q, n=n0 + i: q_consume(q, n))) for i in range(3)] + [mq])
            if gating:
                for qt in range(4):
                    own = 2 * ti + qt // 2
                    self.memset(G[:, qt, :, own:16], -BIG)
                    mb = Mb[qt % 2]
                    mbh = Mb_heads[qt % 2]
                    for h in range(NH_B):
                        self.P.op("dve", lambda e, qt=qt, h=h: e.max(out=top8[h].ap, in_=G.ap[:, qt, h, :]),
                                  reads=G.tks, writes=top8[h].tks)
                    thr = V(top8_all.ap[:, :, 2:3].to_broadcast([128, NH_B, 16]), top8_all.tks)
                    self.tt(mb[:, :, 64:80], G[:, qt], thr, ALU.is_lt)
                    self.ts(mb[:, :, 64:80], mb[:, :, 64:80], -BIG)
                    self.memset(mb[:, :, 64 + own:65 + own], 0.0)
                    for h0 in range(0, NH_B, 4):
                        pt = self.ps("T")
                        for hh in range(4):
                            self.tr(pt[0:80, hh * 128:(hh + 1) * 128], mbh[h0 + hh], ident_b)
                        self.copy(qaug_all[64:80, h0:h0 + 4, qt * 128:(qt + 1) * 128],
                                  pt[64:80, 0:512].rearrange("p (h q) -> p h q", q=128))
            nkeys = (ti + 1) * T
            pending = []
            s3 = [self.psfam["S"][0][0], self.psfam["S"][0][1], self.psfam["B"][0][1]]
            s3i = [0]

            def nxt_s():
                v = s3[s3i[0] % 3]
                s3i[0] += 1
                return v

            def finalize(po, h):
                r1 = self.nxt("r1")
                self.recip_act(r1[64:65, :], po[64:65, :])
                pb = self.psfam["B"][0][0]
                self.mm(pb[0:64, :], ones_f[64:65, 0:64], r1[64:65, :])
                rec = self.nxt("rec")
                self.copy(rec, pb[0:64, :], eng="act")
                put_head(cato[h // 2], h, po[0:64, :], rec)

            npast = 4 * ti
            for h in range(NH_B):
                kt = Kt[h % 2]
                vt = Vt[h % 2]
                self.dma("sp", f"kt{h % 2}", kt[0:64, 0:nkeys], kscr[h][:, 0:nkeys])
                self.dma("sp", f"vt{h % 2}", vt[:, 0:nkeys // 128, 0:HD],
                         V(vscr_t[0:nkeys, h * HD:(h + 1) * HD].rearrange("(k p) d -> p k d", p=128),
                           vscr_tk[0:ti + 1]))
                po = self.ps("A")[0:HD + 1, :]
                items = []
                for j in range(npast):
                    def e_qk(pss, j=j):
                        self.mm(pss, kt[0:80, j * 128:(j + 1) * 128], qaug[h])
                    items.append((e_qk, [(j, slice(0, 512), slice(0, 512))]))
                for qb in range(2):
                    i = 2 * ti + qb
                    q0 = qb * 256
                    for n in range(2 * ti, i + 1):
                        def e_qk(pss, n=n, i=i, q0=q0):
                            for kk in range(2):
                                j = 2 * n + kk
                                ksl = kt[0:80, j * 128:(j + 1) * 128]
                                o = pss[:, kk * 256:(kk + 1) * 256]
                                if n < i:
                                    self.mm(o, ksl, qaug[h][:, q0:q0 + 256])
                                elif kk == 0:
                                    self.mm(o[:, 0:128], ksl, qaug[h][:, q0:q0 + 128], start=True, stop=False)
                                    self.mm(o[:, 0:128], ident_b, tri_b, start=False, stop=True)
                                    self.mm(o[:, 128:256], ksl, qaug[h][:, q0 + 128:q0 + 256])
                                else:
                                    self.mm(o[:, 0:128], ident_b, neg_b)
                                    self.mm(o[:, 128:256], ksl, qaug[h][:, q0 + 128:q0 + 256], start=True, stop=False)
                                    self.mm(o[:, 128:256], ident_b, tri_b, start=False, stop=True)
                        items.append((e_qk, [(2 * n + kk, slice(kk * 256, (kk + 1) * 256), slice(q0, q0 + 256))
                                             for kk in range(2)]))
                nit = len(items)
                npv = sum(len(it[1]) for it in items)
                pss_q = []
                for a in range(min(2, nit)):
                    p_ = nxt_s()
                    items[a][0](p_)
                    pss_q.append(p_)
                kpv = 0
                for a in range(nit):
                    if a + 2 < nit:
                        p_ = nxt_s()
                        items[a + 2][0](p_)
                        pss_q.append(p_)
                    cur = pss_q[a]
                    pT = self.nxt("pT")
                    self.act(pT, cur, AF.Exp)
                    for (j, psl, osl) in items[a][1]:
                        self.mm(po[:, osl], vt[:, j, :], pT[:, psl], start=(kpv == 0), stop=(kpv == npv - 1))
                        kpv += 1
                    if a == 0 and pending:
                        finalize(*pending.pop())
                pending.append((po, h))
            while pending:
                finalize(*pending.pop())
            mem_attn(1)
            wo_proj(1, [(0, 128, 8, cato + catm)])

        def prefetch_x(ti):
            for tt_ in range(4):
                r0 = ti * T + tt_ * 128
                ld(xpre[tt_], x[r0:r0 + 128, :], chan=f"xp{tt_}")

        def load_tile(ti):
            for tt_ in range(4):
                s = xpre[tt_]
                for half in range(2):
                    pt = self.ps("A")
                    for cc in range(4):
                        c = half * 4 + cc
                        self.tr(pt[:, cc * 128:(cc + 1) * 128], s[:, c * 128:(c + 1) * 128], ident_f)
                    for cc in range(4):
                        c = half * 4 + cc
                        self.copy(hT[c][:, tt_ * 128:(tt_ + 1) * 128], pt[:, cc * 128:(cc + 1) * 128],
                                  eng=("act" if cc % 2 else "dve"))
            if ti + 1 < self.n_tiles:
                prefetch_x(ti + 1)

        def store_tile(ti):
            uswitch(xs)
            for tt_ in range(4):
                s = xs[tt_ % 2]
                for half in range(2):
                    pt = self.ps("A")
                    for cc in range(4):
                        c = half * 4 + cc
                        self.tr(pt[:, cc * 128:(cc + 1) * 128], hT[c][:, tt_ * 128:(tt_ + 1) * 128], ident_f)
                    self.copy(s[:, half * 512:(half + 1) * 512], pt, eng=("act" if half else "dve"))
                r0 = ti * T + tt_ * 128
                sp_toks.append(self.dma("sp", f"os{tt_ % 2}", V(out[r0:r0 + 128, :], []), s))

        def mem_kv_init(l):
            uswitch(init_views)
            for mt in range(2):
                ld(memf[:, mt, :], mem[mt * 128:(mt + 1) * 128, :], chan="mem")
            for mt in range(2):
                self.act(junk, memf[:, mt, :], AF.Square, accum=mss[:, mt:mt + 1])
            self.act(mss, mss, AF.Sqrt, bias=float(EPS), scale=1.0 / D)
            self.recip(mss, mss)
            for mt in range(2):
                self.ts(memn[:, mt, :], memf[:, mt, :], mss[:, mt:mt + 1])
            for mt in range(2):
                for c0 in (0, 4):
                    pt = self.ps("T")
                    for cc in range(4):
                        c = c0 + cc
                        self.tr(pt[:, cc * 128:(cc + 1) * 128], memn[:, mt, c * 128:(c + 1) * 128], ident_b)
                    for cc in range(4):
                        c = c0 + cc
                        self.ts(memT[c][:, mt * 128:(mt + 1) * 128], pt[:, cc * 128:(cc + 1) * 128], gmem[:, l, c:c + 1])
            wm = W["w_mem_kv"][l]
            for h in range(NH_M):
                if h % 2 == 0:
                    w = wchunk(wm, h // 2)
                pk = self.ps("A")
                for kc in range(8):
                    self.mm(pk[0:64, 0:256], w[:, kc, (h % 2) * 64:(h % 2) * 64 + 64], memT[kc],
                            start=(kc == 0), stop=(kc == 7))
                qf = self.nxt("qf")[0:64, 0:256]
                sqh = self.nxt("sqh")[0:64, 0:256]
                rr = self.nxt("rr")[0:64, 0:256]
                self.copy(qf, pk[0:64, 0:256], eng="act")
                self.tt(sqh, qf, qf, ALU.mult)
                pss = self.ps("M")
                self.mm(pss[0:64, 0:256], ones_b[0:64, 0:64], sqh)
                self.rsqrt_act(rr, pss[0:64, 0:256], float(HD * EPS))
                self.stt(memK[l][h], qf, hg[0:64, 2 + l:3 + l], rr, ALU.mult, ALU.mult)
            wv = [self.wload(wm[kc0 * 128:(kc0 + 4) * 128, 256:512].rearrange("(kc p) c -> p kc c", p=128), 128, (4, 256))
                  for kc0 in (0, 4)]
            for mt in range(2):
                pvv = self.ps("B")
                for kc in range(8):
                    self.mm(pvv[:, 0:256], memT[kc][:, mt * 128:(mt + 1) * 128], wv[kc // 4][:, kc % 4, :],
                            start=(kc == 0), stop=(kc == 7))
                self.copy(memV[l][:, mt, :], pvv[:, 0:256], eng="act")

        stages = ["ffn1_0", "mix_0", "ffn2_0", "kv", "ffn1_1", "mix_1", "ffn2_1"]
        nst = len(stages) if self.stop_after is None else (0 if self.stop_after == 'none' else stages.index(self.stop_after) + 1)
        prefetch_x(0)
        for ti in range(self.n_tiles):
            load_tile(ti)
            if 'rope' not in _DBG:
                rope_tables(ti)
            if ti == 0:
                late_param_loads()
            for sname in stages[:nst]:
                if sname == "ffn1_0":
                    ffn(0, "ffn1", 0)
                    if ti == 0:
                        late_params()
                        mem_kv_init(0)
                        mem_kv_init(1)
                elif sname == "mix_0":
                    mixer_a()
                elif sname == "ffn2_0":
                    ffn(0, "ffn2", 4)
                elif sname == "kv":
                    kv_stage(ti)
                elif sname == "ffn1_1":
                    ffn(1, "ffn1", 1)
                elif sname == "mix_1":
                    mixer_b(ti)
                elif sname == "ffn2_1":
                    ffn(1, "ffn2", 5)
            store_tile(ti)
        self.P.wait_all("sp", sp_toks)
        self.P.emit(st)
        st.close()
        return nc


def _consts():
    ident = np.eye(128, dtype=np.float32)
    ones = np.ones((128, 128), np.float32)
    k = np.arange(128)[:, None]
    q = np.arange(128)[None, :]
    tri = np.where(k <= q, 0.0, -BIG).astype(np.float32)
    neg = np.full((128, 128), -BIG, np.float32)
    perm = np.zeros((128, 128), np.float32)
    onesblk = np.zeros((128, 128), np.float32)
    rope = np.zeros((128, 2), np.float32)
    inv_freq = 1.0 / (500000.0 ** (np.arange(0, 16, 2, dtype=np.float32) / 16.0))
    for b0 in (0, 64):
        onesblk[b0:b0 + 64, b0:b0 + 64] = 1.0
        for d in range(8):
            perm[b0 + d + 8, b0 + d] = 1.0
            perm[b0 + d, b0 + d + 8] = 1.0
        for d in range(16):
            rope[b0 + d, 0] = np.float32(inv_freq[d % 8]) / np.float32(2.0 * math.pi)
            rope[b0 + d, 1] = -1.0 if d < 8 else 1.0
    ind = np.zeros((16, S), np.float32)
    for n in range(16):
        ind[n, n * 256:(n + 1) * 256] = 1.0
    return {"c_ident": ident, "c_ones": ones, "c_tri": tri, "c_neg": neg, "c_perm": perm, "c_onesblk": onesblk,
            "c_rope": rope, "c_ind": ind}


_WNAMES = ["ffn1_norm_g", "ffn1_w_gate", "ffn1_w_up", "ffn1_w_down", "mix_norm_g", "mem_norm_g", "w_mem_kv",
           "mem_q_norm_g", "mem_k_norm_g", "w_o", "ffn2_norm_g", "ffn2_w_gate", "ffn2_w_up", "ffn2_w_down",
           "a_w_in", "a_dw_kernel", "a_dw_bias", "a_ln_g", "a_ln_b", "kv_norm_g", "w_kv", "k_norm_g",
           "b_w_in", "b_q_norm_g"]


def kernel(stop_after=None, n_cores=8, n_tiles=NT, skip_init=False, **inputs):
    nc = Builder(stop_after=stop_after, n_tiles=n_tiles, skip_init=skip_init).build()
    consts = _consts()
    shared = {n: np.ascontiguousarray(np.asarray(inputs[n], dtype=np.float32)) for n in _WNAMES}
    shared.update(consts)
    x = np.asarray(inputs["x"], dtype=np.float32)
    mem = np.asarray(inputs["mem"], dtype=np.float32)
    pos = np.asarray(inputs["positions"], dtype=np.int32)
    in_maps = []
    for b in range(n_cores):
        m = dict(shared)
        m["x"] = np.ascontiguousarray(x[b])
        m["mem"] = np.ascontiguousarray(mem[b])
        m["pos"] = np.ascontiguousarray(pos[b:b + 1])
        in_maps.append(m)
    res = run_bass_kernel_spmd(nc, in_maps, core_ids=list(range(n_cores)))
    return np.stack([np.asarray(r["out"], dtype=np.float32) for r in res.results], axis=0)
```
